# Optimizing a Trainium2 kernel written in Bass

```python
import math
import jax, jax.numpy as jnp
from jax import lax
import numpy as np

D_MODEL = 4096
BATCH = 8
SEQ = 2048
DEPTH = 4

N_HEADS = 8
HEAD_DIM = 128
ATTN_W = N_HEADS * 2 * HEAD_DIM
Q_BLOCK = 128
LRU_W = 1024
LRU_BLOCKS = 8
LRU_BLOCK_W = LRU_W // LRU_BLOCKS
CONV_W = 4
LRU_C = 8.0
D_FF = 4 * D_MODEL
EPS = 1e-6
C_IN = 3 * ATTN_W + 2 * LRU_W + 2 * D_MODEL

kernel_name = "hybrid_diffattn_rglru_gated_block"


def rmsnorm(x, g):
    xf = x.astype(jnp.float32)
    y = xf * lax.rsqrt(jnp.mean(xf * xf, axis=-1, keepdims=True) + EPS)
    return (y * g.astype(jnp.float32)).astype(x.dtype)


def diff_attention(q, k, v, lam):
    B, S = q.shape[0], q.shape[1]
    nqb = S // Q_BLOCK
    qf = q.astype(jnp.float32) * (HEAD_DIM ** -0.5)
    kf = k.astype(jnp.float32)
    vf = v.astype(jnp.float32)
    qb = qf.reshape(B, nqb, Q_BLOCK, N_HEADS, 2, HEAD_DIM).transpose(1, 0, 2, 3, 4, 5)
    k_pos = jnp.arange(S)

    def one_block(args):
        q_blk, i = args
        s = jnp.einsum('bqhcd,bkhcd->bhcqk', q_blk, kf)
        q_pos = i * Q_BLOCK + jnp.arange(Q_BLOCK)
        mask = q_pos[:, None] >= k_pos[None, :]
        p = jax.nn.softmax(jnp.where(mask, s, -jnp.inf), axis=-1)
        w = p[:, :, 0] - lam * p[:, :, 1]
        return jnp.einsum('bhqk,bkhe->bqhe', w, vf)

    out = lax.map(one_block, (qb, jnp.arange(nqb)))
    return out.transpose(1, 0, 2, 3, 4).reshape(B, S, N_HEADS, 2 * HEAD_DIM)


def causal_conv(x, w, b):
    y = lax.conv_general_dilated(
        x, w[:, None, :].astype(x.dtype), window_strides=(1,), padding=[(CONV_W - 1, 0)],
        dimension_numbers=('NWC', 'WIO', 'NWC'), feature_group_count=x.shape[-1])
    return y + b.astype(x.dtype)


def rg_lru(x, w_r, b_r, w_i, b_i, lam):
    B, S, R = x.shape
    xf = x.astype(jnp.float32)
    xb = xf.reshape(B, S, LRU_BLOCKS, LRU_BLOCK_W)
    r = jax.nn.sigmoid(jnp.einsum('bsnc,ncd->bsnd', xb, w_r.astype(jnp.float32)).reshape(B, S, R)
                       + b_r.astype(jnp.float32))
    i = jax.nn.sigmoid(jnp.einsum('bsnc,ncd->bsnd', xb, w_i.astype(jnp.float32)).reshape(B, S, R)
                       + b_i.astype(jnp.float32))
    log_a = -LRU_C * r * jax.nn.softplus(-lam.astype(jnp.float32))
    a = jnp.exp(log_a)
    bt = jnp.sqrt(-jnp.expm1(2.0 * log_a)) * (i * xf)

    def combine(left, right):
        a1, b1 = left
        a2, b2 = right
        return a1 * a2, a2 * b1 + b2

    _, h = lax.associative_scan(combine, (a, bt), axis=1)
    return h


def setup_inputs(seed: int = 0) -> dict:
    key = jax.random.key(seed)
    ks = jax.random.split(key, 24)
    f32 = jnp.float32
    nrm = lambda k, shape, scale: jax.random.normal(k, shape, f32) * scale
    u = jax.random.uniform(ks[13], (DEPTH, LRU_W), f32, 0.9, 0.999)
    p = u ** (1.0 / LRU_C)
    return {
        "x": jax.random.normal(ks[0], (BATCH, SEQ, D_MODEL), f32),
        "norm1_g": 1.0 + nrm(ks[1], (DEPTH, D_MODEL), 0.01),
        "w_in": nrm(ks[2], (DEPTH, D_MODEL, C_IN), D_MODEL ** -0.5),
        "gate_b": nrm(ks[3], (DEPTH, 2, D_MODEL), 0.01),
        "lam_qk": nrm(ks[4], (DEPTH, 4, HEAD_DIM), 0.1),
        "subln_g": 1.0 + nrm(ks[5], (DEPTH, 2 * HEAD_DIM), 0.01),
        "w_attn_proj": nrm(ks[6], (DEPTH, ATTN_W, D_MODEL), ATTN_W ** -0.5),
        "conv_w": nrm(ks[7], (DEPTH, CONV_W, LRU_W), CONV_W ** -0.5),
        "conv_b": nrm(ks[8], (DEPTH, LRU_W), 0.01),
        "w_rgate": nrm(ks[9], (DEPTH, LRU_BLOCKS, LRU_BLOCK_W, LRU_BLOCK_W), LRU_BLOCK_W ** -0.5),
        "b_rgate": nrm(ks[10], (DEPTH, LRU_W), 0.01),
        "w_igate": nrm(ks[11], (DEPTH, LRU_BLOCKS, LRU_BLOCK_W, LRU_BLOCK_W), LRU_BLOCK_W ** -0.5),
        "b_igate": nrm(ks[12], (DEPTH, LRU_W), 0.01),
        "lru_lambda": jnp.log(p) - jnp.log1p(-p),
        "w_rec_proj": nrm(ks[14], (DEPTH, LRU_W, D_MODEL), LRU_W ** -0.5),
        "w_out": nrm(ks[15], (DEPTH, D_MODEL, D_MODEL), D_MODEL ** -0.5),
        "norm2_g": 1.0 + nrm(ks[16], (DEPTH, D_MODEL), 0.01),
        "w_up": nrm(ks[17], (DEPTH, D_MODEL, D_FF), D_MODEL ** -0.5),
        "w_down": nrm(ks[18], (DEPTH, D_FF, D_MODEL), D_FF ** -0.5),
        "final_g": 1.0 + nrm(ks[19], (D_MODEL,), 0.01),
    }


def reference(x, norm1_g, w_in, gate_b, lam_qk, subln_g, w_attn_proj, conv_w, conv_b,
              w_rgate, b_rgate, w_igate, b_igate, lru_lambda, w_rec_proj, w_out,
              norm2_g, w_up, w_down, final_g):
    B, S, _ = x.shape
    splits = np.cumsum([ATTN_W, ATTN_W, ATTN_W, LRU_W, LRU_W, D_MODEL]).tolist()
    for l in range(DEPTH):
        h = rmsnorm(x, norm1_g[l])
        z = h @ w_in[l]
        zq, zk, zv, zlx, zly, zga, zgr = jnp.split(z, splits, axis=-1)
        lam_init = 0.8 - 0.6 * math.exp(-0.3 * l)
        lq = lam_qk[l].astype(jnp.float32)
        lam = jnp.exp(jnp.sum(lq[0] * lq[1])) - jnp.exp(jnp.sum(lq[2] * lq[3])) + lam_init
        q = zq.reshape(B, S, N_HEADS, 2, HEAD_DIM)
        k = zk.reshape(B, S, N_HEADS, 2, HEAD_DIM)
        v = zv.reshape(B, S, N_HEADS, 2 * HEAD_DIM)
        att = diff_attention(q, k, v, lam)
        att = rmsnorm(att, subln_g[l]) * (1.0 - lam_init)
        y_att = att.reshape(B, S, ATTN_W).astype(x.dtype) @ w_attn_proj[l]
        xr = causal_conv(zlx, conv_w[l], conv_b[l])
        hr = rg_lru(xr, w_rgate[l], b_rgate[l], w_igate[l], b_igate[l], lru_lambda[l])
        rec = (hr * jax.nn.gelu(zly.astype(jnp.float32))).astype(x.dtype)
        y_rec = rec @ w_rec_proj[l]
        g_att = jax.nn.sigmoid(zga + gate_b[l, 0])
        g_rec = jax.nn.sigmoid(zgr + gate_b[l, 1])
        x = x + (g_att * y_att + g_rec * y_rec) @ w_out[l]
        h2 = rmsnorm(x, norm2_g[l])
        x = x + jnp.square(jax.nn.relu(h2 @ w_up[l])) @ w_down[l]
    return rmsnorm(x, final_g)
```

```python
import math
from contextlib import ExitStack
import numpy as np
import concourse.bass as bass
import concourse.mybir as mybir
from concourse.bass_utils import run_bass_kernel_spmd

F32 = mybir.dt.float32
BF16 = mybir.dt.bfloat16
U8 = mybir.dt.uint8
AF = mybir.ActivationFunctionType
ALU = mybir.AluOpType
AX = mybir.AxisListType

S = 2048
D = 4096
T = 512
AW = 2048
RW = 1024
FF = 16384
CIN = 16384
EPS = 1e-6
NCH = D // 128
PPL = 194
KB = 1024


class Sched:
    def __init__(self, nc, stack):
        self.nc = nc
        self.engs = ['pe', 'act', 'dve', 'sp', 'gq']
        self.ops = {e: [] for e in self.engs}
        self.cnt = {e: 0 for e in self.engs}
        self.sem = {e: stack.enter_context(nc.semaphore('s_' + e)) for e in ['pe', 'act', 'dve']}
        self.NDS = 16
        self.dsem = {q: [stack.enter_context(nc.semaphore('d_%s%d' % (q, i))) for i in range(self.NDS)]
                     for q in ['sp', 'gq']}
        self.dval = {q: [0] * self.NDS for q in ['sp', 'gq']}
        self.dnext = {q: 0 for q in ['sp', 'gq']}
        self.waited = {e: {} for e in self.engs}
        self.lastw = {}
        self.readers = {}

    def op(self, eng, fn, reads=(), writes=()):
        deps = {}

        def add(tok):
            if tok is None:
                return
            teng, sem, val = tok
            if teng == 'pe' and eng == 'pe':
                return
            key = id(sem)
            if key not in deps or deps[key][1] < val:
                deps[key] = (sem, val)
        for p in reads:
            add(self.lastw.get(p))
        for p in writes:
            add(self.lastw.get(p))
            for t in self.readers.get(p, {}).values():
                add(t)
        if eng in ('sp', 'gq'):
            k = self.dnext[eng]
            self.dnext[eng] = (k + 1) % self.NDS
            sem = self.dsem[eng][k]
            prev = self.dval[eng][k]
            if prev > 0:
                add(('dma_prev', sem, prev))
            self.dval[eng][k] = prev + 16
            tok = (eng, sem, prev + 16)
            inc = 16
        else:
            self.cnt[eng] += 1
            sem = self.sem[eng]
            tok = (eng, sem, self.cnt[eng])
            inc = 1
        waits = []
        w = self.waited[eng]
        for key, (s, v) in deps.items():
            if w.get(key, 0) >= v:
                continue
            w[key] = v
            waits.append((s, v))
        self.ops[eng].append((waits, fn, sem, inc))
        for p in writes:
            self.lastw[p] = tok
            self.readers[p] = {}
        for p in reads:
            self.readers.setdefault(p, {})[id(tok[1])] = tok
        return tok

    def emit(self, block):
        def mk(name):
            def body(e):
                for waits, fn, sem, inc in self.ops[name]:
                    for s, v in waits:
                        e.wait_ge(s, v)
                    ins = fn(e)
                    ins.then_inc(sem, inc)
                if name == 'sp':
                    for q in ('sp', 'gq'):
                        for k in range(self.NDS):
                            if self.dval[q][k] > 0:
                                e.wait_ge(self.dsem[q][k], self.dval[q][k])
            return body
        block.tensor(mk('pe'))
        block.scalar(mk('act'))
        block.vector(mk('dve'))
        block.sync(mk('sp'))
        block.gpsimd(mk('gq'))


class Buf:
    def __init__(self, arena, off, shape, dtype):
        esz = 4 if dtype == F32 else 2
        n = 1
        for s in shape:
            n *= s
        self.off = off
        self.nbytes = n * esz
        ap = arena[:, off:off + n * esz].bitcast(dtype)
        if len(shape) == 2:
            ap = ap.rearrange("p (a b) -> p a b", a=shape[0])
        elif len(shape) == 3:
            ap = ap.rearrange("p (a b c) -> p a b c", a=shape[0], b=shape[1])
        self.ap = ap
        self.shape = shape

    def pages(self, lo=None, hi=None):
        lo = 0 if lo is None else lo
        hi = self.nbytes if hi is None else hi
        return [('a', i) for i in range((self.off + lo) // KB, (self.off + hi - 1) // KB + 1)]

    def cp(self, c, n=1):
        per = self.nbytes // self.shape[0]
        return self.pages(c * per, (c + n) * per)


def build(L, NT, n_in_layers=None):
    nc = bass.Bass("TRN2", target_bir_lowering=False)
    LW = L if n_in_layers is None else n_in_layers
    x_d = nc.dram_tensor("x", [S, D], F32, kind="ExternalInput").ap()
    w_in_d = nc.dram_tensor("w_in", [LW, D, CIN], F32, kind="ExternalInput").ap()
    w_ap_d = nc.dram_tensor("w_attn_proj", [LW, AW, D], F32, kind="ExternalInput").ap()
    w_rp_d = nc.dram_tensor("w_rec_proj", [LW, RW, D], F32, kind="ExternalInput").ap()
    w_out_d = nc.dram_tensor("w_out", [LW, D, D], F32, kind="ExternalInput").ap()
    w_up_d = nc.dram_tensor("w_up", [LW, D, FF], F32, kind="ExternalInput").ap()
    w_dn_d = nc.dram_tensor("w_down", [LW, FF, D], F32, kind="ExternalInput").ap()
    w_rg_d = nc.dram_tensor("w_rgate", [LW, 8, 128, 128], F32, kind="ExternalInput").ap()
    w_ig_d = nc.dram_tensor("w_igate", [LW, 8, 128, 128], F32, kind="ExternalInput").ap()
    pp_d = nc.dram_tensor("pp", [128, PPL * 4 + 32], F32, kind="ExternalInput").ap()
    lamqk_d = nc.dram_tensor("lam_qk", [4, 4, 128], F32, kind="ExternalInput").ap()
    consts_d = nc.dram_tensor("consts", [128, 384], F32, kind="ExternalInput").ap()
    out_d = nc.dram_tensor("out", [S, D], F32, kind="ExternalOutput").ap()
    kT_d = nc.dram_tensor("kT_cache", [L, AW, S], BF16).ap()
    vC_d = nc.dram_tensor("v_cache", [L, S, AW], BF16).ap()

    with ExitStack() as st:
        ARENA = 198 * KB
        arena = st.enter_context(nc.sbuf_tensor("arena", [128, ARENA], U8))
        ident = st.enter_context(nc.sbuf_tensor("ident", [128, 384], F32))
        cbf = st.enter_context(nc.sbuf_tensor("cbf", [128, 256], BF16))
        pp = st.enter_context(nc.sbuf_tensor("ppt", [128, PPL * 4 + 32], F32))
        lamt = st.enter_context(nc.sbuf_tensor("lamt", [128, 16], F32))
        hst = st.enter_context(nc.sbuf_tensor("hst", [128, 4 * 8], F32))
        chist = st.enter_context(nc.sbuf_tensor("chist", [128, 4 * 8 * 3], F32))
        ps = st.enter_context(nc.psum_tensor("ps", [128, 8, 512], F32))
        sc = Sched(nc, st)
        block = st.enter_context(nc.Block())

        identf = ident[:, 0:128]
        onesf = ident[:, 256:384]
        tri_bf = cbf[:, 0:128]
        ones_bf = cbf[:, 128:256]

        xacc = Buf(arena, 0, (32, 512), F32)
        hT = Buf(arena, 64 * KB, (32, 512), BF16)
        mT = Buf(arena, 96 * KB, (32, 512), BF16)
        kbuf = [Buf(arena, 96 * KB + s * 8 * KB, (2, 2048), BF16) for s in range(2)]
        vbuf = [Buf(arena, 112 * KB + s * 8 * KB, (16, 256), BF16) for s in range(2)]
        xstage = [Buf(arena, 96 * KB + s * 16 * KB, (32, 128), F32) for s in range(2)]
        vst = Buf(arena, 96 * KB, (4, 2048), BF16)
        wg = Buf(arena, 96 * KB, (2, 8, 128), BF16)
        gel = Buf(arena, 104 * KB, (8, 512), BF16)
        ltmp = [Buf(arena, 112 * KB + i * 2 * KB, (512,), F32) for i in range(8)]
        qT = Buf(arena, 128 * KB, (16, 512), BF16)
        NWB = 3
        wb = [Buf(arena, 144 * KB + s * 8 * KB, (8, 512), BF16) for s in range(NWB)]
        rec = Buf(arena, 168 * KB, (8, 512), BF16)
        zlx = Buf(arena, 176 * KB, (8, 515), F32)
        kst = Buf(arena, 176 * KB, (16, 512), BF16)
        pT = [Buf(arena, 176 * KB + i * KB, (512,), BF16) for i in range(4)]
        r1 = Buf(arena, 180 * KB, (512,), F32)
        r2 = Buf(arena, 182 * KB, (512,), F32)
        att = Buf(arena, 184 * KB, (2, 512), F32)
        t1b = Buf(arena, 188 * KB, (2, 512), F32)
        sgate = Buf(arena, 176 * KB, (4, 512), BF16)
        mtmp = Buf(arena, 180 * KB, (4, 512), F32)
        rl = [Buf(arena, 176 * KB + i * 2 * KB, (512,), F32) for i in range(2)]
        sq = [Buf(arena, 192 * KB + i * 2 * KB, (512,), F32) for i in range(2)]
        rstd = Buf(arena, 196 * KB, (512,), F32)
        lqb = Buf(arena, 0, (2048,), F32)

        PSB = lambda b, n=1: [('ps', i) for i in range(b, b + n)]
        PP = [('pp',)]
        CONST = [('const',)]

        sc.op('sp', lambda e: e.dma_start(out=ident[:], in_=consts_d[:, :]), writes=CONST)
        sc.op('sp', lambda e: e.dma_start(out=pp[:], in_=pp_d[:, :]), writes=PP)
        sc.op('sp', lambda e: e.dma_start(
            out=lqb.ap, in_=lamqk_d.rearrange("l f d -> (l f d)").partition_broadcast(128)),
            writes=lqb.pages())
        sc.op('dve', lambda e: e.tensor_copy(out=cbf[:, 0:128], in_=ident[:, 128:256]), reads=CONST, writes=[('cbf',)])
        sc.op('dve', lambda e: e.tensor_copy(out=cbf[:, 128:256], in_=ident[:, 256:384]), reads=CONST, writes=[('cbf',)])
        CBF = [('cbf',)]
        sc.op('dve', lambda e: e.memset(hst[:], 0.0), writes=[('hst',)])
        sc.op('dve', lambda e: e.memset(chist[:], 0.0), writes=[('chist',)])
        for l in range(L):
            b = l * PPL
            lam_init = 0.8 - 0.6 * math.exp(-0.3 * l)
            sc.op('dve', lambda e, b=b: e.tensor_scalar(out=pp[:, b:b + 64], in0=pp[:, b:b + 64], scalar1=EPS ** -0.5,
                                                        scalar2=None, op0=ALU.mult), reads=PP, writes=PP)
            sc.op('act', lambda e, b=b: e.activation(out=pp[:, b + 184:b + 192], in_=pp[:, b + 184:b + 192],
                                                     func=AF.Exp, scale=-1.0), reads=PP, writes=PP)
            sc.op('act', lambda e, b=b: e.activation(out=pp[:, b + 184:b + 192], in_=pp[:, b + 184:b + 192],
                                                     func=AF.Ln, bias=1.0, scale=1.0), reads=PP, writes=PP)
            sc.op('dve', lambda e, b=b: e.tensor_scalar(out=pp[:, b + 184:b + 192], in0=pp[:, b + 184:b + 192],
                                                        scalar1=-8.0, scalar2=None, op0=ALU.mult), reads=PP, writes=PP)
            sc.op('dve', lambda e, b=b, c=(EPS ** -0.5) * (1.0 - lam_init): e.tensor_scalar(
                out=pp[:, b + 192:b + 194], in0=pp[:, b + 192:b + 194], scalar1=c, scalar2=None, op0=ALU.mult),
                reads=PP, writes=PP)
            LT = [('lamt',)]
            for hf in range(2):
                o0 = l * 512 + hf * 256
                sc.op('dve', lambda e, o0=o0: e.tensor_tensor(out=lqb.ap[:, o0:o0 + 128], in0=lqb.ap[:, o0:o0 + 128],
                                                              in1=lqb.ap[:, o0 + 128:o0 + 256], op=ALU.mult),
                      reads=lqb.pages(), writes=lqb.pages())
                sc.op('dve', lambda e, o0=o0, l=l, hf=hf: e.reduce_sum(out=lamt[:, l * 4 + hf:l * 4 + hf + 1],
                                                                      in_=lqb.ap[:, o0:o0 + 128], axis=AX.X),
                      reads=lqb.pages(), writes=LT)
            sc.op('act', lambda e, l=l: e.activation(out=lamt[:, l * 4:l * 4 + 2], in_=lamt[:, l * 4:l * 4 + 2],
                                                     func=AF.Exp), reads=LT, writes=LT)
            sc.op('dve', lambda e, l=l: e.tensor_tensor(out=lamt[:, l * 4 + 2:l * 4 + 3], in0=lamt[:, l * 4:l * 4 + 1],
                                                        in1=lamt[:, l * 4 + 1:l * 4 + 2], op=ALU.subtract),
                  reads=LT, writes=LT)
            sc.op('dve', lambda e, l=l, li=lam_init: e.tensor_scalar(
                out=lamt[:, l * 4 + 3:l * 4 + 4], in0=lamt[:, l * 4 + 2:l * 4 + 3], scalar1=li, scalar2=-1.0,
                op0=ALU.add, op1=ALU.mult), reads=LT, writes=LT)
        fb = 4 * PPL
        sc.op('dve', lambda e: e.tensor_scalar(out=pp[:, fb:fb + 32], in0=pp[:, fb:fb + 32], scalar1=EPS ** -0.5,
                                               scalar2=None, op0=ALU.mult), reads=PP, writes=PP)

        state = {'wslot': 0, 'psg': 0, 'cp': 0}

        def evac_eng():
            state['cp'] ^= 1
            return 'act' if state['cp'] else 'dve'

        def copy_op(eng, out, in_, reads, writes, scale=None):
            if eng == 'act':
                if scale is None:
                    sc.op('act', lambda e: e.activation(out=out, in_=in_, func=AF.Copy), reads=reads, writes=writes)
                else:
                    sc.op('act', lambda e: e.activation(out=out, in_=in_, func=AF.Copy, scale=scale),
                          reads=reads, writes=writes)
            else:
                if scale is None:
                    sc.op('dve', lambda e: e.tensor_copy(out=out, in_=in_), reads=reads, writes=writes)
                else:
                    sc.op('dve', lambda e: e.tensor_scalar(out=out, in0=in_, scalar1=scale, scalar2=None,
                                                           op0=ALU.mult), reads=reads, writes=writes)

        def gemm(wsrc, nk, act_chunk, act_pages, evac, mode='B'):
            g = state['psg']
            state['psg'] ^= 1
            b0 = 4 * g
            for kb in range(nk):
                s = state['wslot']
                state['wslot'] = (s + 1) % NWB
                src = wsrc(kb).rearrange("(kc p) c -> p kc c", p=128)
                sc.op('gq', lambda e, s=s, src=src: e.dma_start(out=wb[s].ap, in_=src), writes=wb[s].pages())

                def mm(e, s=s, kb=kb):
                    last = None
                    for kc in range(8):
                        k = kb * 8 + kc
                        for n in range(4):
                            if mode == 'B':
                                last = e.matmul(ps[:, b0 + n, :], wb[s].ap[:, kc, n * 128:(n + 1) * 128],
                                                act_chunk(k), start=(k == 0), stop=(k == nk * 8 - 1))
                            else:
                                last = e.matmul(ps[:, b0 + n, :], act_chunk(k)[:, n * 128:(n + 1) * 128],
                                                wb[s].ap[:, kc, :], start=(k == 0), stop=(k == nk * 8 - 1))
                    return last
                sc.op('pe', mm, reads=wb[s].pages() + act_pages(kb), writes=PSB(b0, 4))
            evac(b0)

        def rmsnorm_to_hT(gcol):
            bank = 4 * state['psg']
            state['psg'] ^= 1
            for c in range(NCH):
                s = sq[c % 2]
                sc.op('act', lambda e, c=c, s=s: e.activation(out=s.ap, in_=xacc.ap[:, c, :], func=AF.Square),
                      reads=xacc.cp(c), writes=s.pages())
                sc.op('pe', lambda e, c=c, s=s: e.matmul(ps[:, bank, :], onesf, s.ap, start=(c == 0),
                                                         stop=(c == NCH - 1)),
                      reads=s.pages() + CONST, writes=PSB(bank))
            sc.op('act', lambda e: e.activation(out=rstd.ap, in_=ps[:, bank, :], func=AF.Sqrt, bias=1.0,
                                                scale=1.0 / (D * EPS)), reads=PSB(bank), writes=rstd.pages())
            sc.op('dve', lambda e: e.reciprocal(out=rstd.ap, in_=rstd.ap), reads=rstd.pages(), writes=rstd.pages())
            for c in range(NCH):
                sc.op('dve', lambda e, c=c: e.scalar_tensor_tensor(
                    out=hT.ap[:, c, :], in0=xacc.ap[:, c, :], scalar=pp[:, gcol + c:gcol + c + 1], in1=rstd.ap,
                    op0=ALU.mult, op1=ALU.mult), reads=xacc.cp(c) + rstd.pages() + PP, writes=hT.cp(c))

        hT_chunk = lambda k: hT.ap[:, k, :]
        hT_pages = lambda kb: hT.cp(kb * 8, 8)
        mT_chunk = lambda k: mT.ap[:, k, :]
        mT_pages = lambda kb: mT.cp(kb * 8, 8)

        for j in range(NT):
            t0 = j * T
            for tb in range(4):
                xs = xstage[tb % 2]
                r0 = t0 + tb * 128
                sc.op('sp', lambda e, xs=xs, r0=r0: e.dma_start(
                    out=xs.ap.rearrange("p a b -> p (a b)"), in_=x_d[r0:r0 + 128, :]), writes=xs.pages())
                for c4 in range(8):
                    bank = (c4 % 2) * 4 + (c4 // 2) % 4

                    def tr(e, xs=xs, c4=c4, bank=bank):
                        last = None
                        for q in range(4):
                            last = e.transpose(ps[:, bank, q * 128:(q + 1) * 128], xs.ap[:, c4 * 4 + q, :], identf)
                        return last
                    sc.op('pe', tr, reads=xs.pages() + CONST, writes=PSB(bank))
                    eng = evac_eng()
                    copy_op(eng, xacc.ap[:, c4 * 4:(c4 + 1) * 4, tb * 128:(tb + 1) * 128],
                            ps[:, bank, :].rearrange("p (a b) -> p a b", a=4),
                            reads=PSB(bank), writes=xacc.cp(c4 * 4, 4))

            for l in range(L):
                b = l * PPL
                rmsnorm_to_hT(b)

                sc.op('gq', lambda e, l=l: e.dma_start(out=wg.ap[:, 0], in_=w_rg_d[l].rearrange("n c d -> c n d")),
                      writes=wg.pages())
                sc.op('gq', lambda e, l=l: e.dma_start(out=wg.ap[:, 1], in_=w_ig_d[l].rearrange("n c d -> c n d")),
                      writes=wg.pages())
                CH = [('chist',)]
                HS = [('hst',)]
                sc.op('dve', lambda e, l=l: e.tensor_copy(
                    out=zlx.ap[:, :, 0:3], in_=chist[:, l * 24:(l + 1) * 24].rearrange("p (a b) -> p a b", a=8)),
                    reads=CH, writes=zlx.pages())
                for cg in range(2):
                    c0 = 3 * AW + cg * 512

                    def ev_lx(b0, cg=cg):
                        for n in range(4):
                            ch = cg * 4 + n
                            copy_op(evac_eng(), zlx.ap[:, ch, 3:515], ps[:, b0 + n, :], reads=PSB(b0 + n),
                                    writes=zlx.pages())
                    gemm(lambda kb, c0=c0, l=l: w_in_d[l, kb * 1024:(kb + 1) * 1024, c0:c0 + 512], 4,
                         hT_chunk, hT_pages, ev_lx)
                for cg in range(2):
                    c0 = 3 * AW + RW + cg * 512

                    def ev_ly(b0, cg=cg):
                        for n in range(4):
                            ch = cg * 4 + n
                            ta, tb_ = ltmp[(n % 2) * 2], ltmp[(n % 2) * 2 + 1]
                            P = PSB(b0 + n)
                            sc.op('act', lambda e, ta=ta, b0=b0, n=n: e.activation(
                                out=ta.ap, in_=ps[:, b0 + n, :], func=AF.Square), reads=P, writes=ta.pages())
                            sc.op('dve', lambda e, ta=ta: e.tensor_scalar(
                                out=ta.ap, in0=ta.ap, scalar1=0.044715, scalar2=1.0, op0=ALU.mult, op1=ALU.add),
                                reads=ta.pages(), writes=ta.pages())
                            sc.op('dve', lambda e, ta=ta, b0=b0, n=n: e.tensor_tensor(
                                out=ta.ap, in0=ta.ap, in1=ps[:, b0 + n, :], op=ALU.mult),
                                reads=ta.pages() + P, writes=ta.pages())
                            sc.op('act', lambda e, ta=ta, tb_=tb_: e.activation(
                                out=tb_.ap, in_=ta.ap, func=AF.Sigmoid, scale=1.5957691216057308),
                                reads=ta.pages(), writes=tb_.pages())
                            sc.op('dve', lambda e, tb_=tb_, b0=b0, n=n, ch=ch: e.tensor_tensor(
                                out=gel.ap[:, ch, :], in0=tb_.ap, in1=ps[:, b0 + n, :], op=ALU.mult),
                                reads=tb_.pages() + P, writes=gel.cp(ch))
                    gemm(lambda kb, c0=c0, l=l: w_in_d[l, kb * 1024:(kb + 1) * 1024, c0:c0 + 512], 4,
                         hT_chunk, hT_pages, ev_ly)
                for n in range(8):
                    xr, xrb, tr_, ti, ta, tm, th, tx = ltmp
                    cw = lambda jj, n=n, b=b: pp[:, b + 128 + jj * 8 + n:b + 128 + jj * 8 + n + 1]
                    col = lambda o, n=n, b=b: pp[:, b + o + n:b + o + n + 1]
                    ZP = zlx.pages()
                    sc.op('dve', lambda e, n=n, cw=cw, col=col: e.tensor_scalar(
                        out=xr.ap, in0=zlx.ap[:, n, 0:512], scalar1=cw(0), scalar2=col(160), op0=ALU.mult, op1=ALU.add),
                        reads=ZP + PP, writes=xr.pages())
                    for jj in range(1, 4):
                        sc.op('dve', lambda e, n=n, jj=jj, cw=cw: e.scalar_tensor_tensor(
                            out=xr.ap, in0=zlx.ap[:, n, jj:jj + 512], scalar=cw(jj), in1=xr.ap, op0=ALU.mult, op1=ALU.add),
                            reads=ZP + PP + xr.pages(), writes=xr.pages())
                    sc.op('act', lambda e: e.activation(out=xrb.ap[:, 0:256].bitcast(BF16), in_=xr.ap, func=AF.Copy),
                          reads=xr.pages(), writes=xrb.pages())
                    xrb16 = xrb.ap[:, 0:256].bitcast(BF16)
                    bank = 4 * state['psg']
                    state['psg'] ^= 1
                    sc.op('pe', lambda e, n=n, bank=bank, xrb16=xrb16: e.matmul(
                        ps[:, bank, :], wg.ap[:, 0, n, :], xrb16, start=True, stop=True),
                        reads=wg.pages() + xrb.pages(), writes=PSB(bank))
                    sc.op('pe', lambda e, n=n, bank=bank, xrb16=xrb16: e.matmul(
                        ps[:, bank + 1, :], wg.ap[:, 1, n, :], xrb16, start=True, stop=True),
                        reads=wg.pages() + xrb.pages(), writes=PSB(bank + 1))
                    sc.op('act', lambda e, bank=bank, col=col: e.activation(
                        out=tr_.ap, in_=ps[:, bank, :], func=AF.Sigmoid, bias=col(168), scale=1.0),
                        reads=PSB(bank) + PP, writes=tr_.pages())
                    sc.op('act', lambda e, bank=bank, col=col: e.activation(
                        out=ti.ap, in_=ps[:, bank + 1, :], func=AF.Sigmoid, bias=col(176), scale=1.0),
                        reads=PSB(bank + 1) + PP, writes=ti.pages())
                    sc.op('act', lambda e, col=col: e.activation(out=ta.ap, in_=tr_.ap, func=AF.Exp, scale=col(184)),
                          reads=tr_.pages() + PP, writes=ta.pages())
                    sc.op('dve', lambda e: e.tensor_tensor(out=tm.ap, in0=ta.ap, in1=ta.ap, op=ALU.mult),
                          reads=ta.pages(), writes=tm.pages())
                    sc.op('act', lambda e: e.activation(out=tm.ap, in_=tm.ap, func=AF.Sqrt, bias=1.0, scale=-1.0),
                          reads=tm.pages(), writes=tm.pages())
                    sc.op('dve', lambda e: e.tensor_tensor(out=ti.ap, in0=ti.ap, in1=xr.ap, op=ALU.mult),
                          reads=ti.pages() + xr.pages(), writes=ti.pages())
                    sc.op('dve', lambda e: e.tensor_tensor(out=tm.ap, in0=tm.ap, in1=ti.ap, op=ALU.mult),
                          reads=ti.pages() + tm.pages(), writes=tm.pages())
                    sc.op('dve', lambda e, l=l, n=n: e.tensor_tensor_scan(
                        out=th.ap, data0=ta.ap, data1=tm.ap, initial=hst[:, l * 8 + n:l * 8 + n + 1],
                        op0=ALU.mult, op1=ALU.add), reads=ta.pages() + tm.pages() + HS, writes=th.pages())
                    sc.op('dve', lambda e, l=l, n=n: e.tensor_copy(out=hst[:, l * 8 + n:l * 8 + n + 1],
                                                                   in_=th.ap[:, 511:512]),
                          reads=th.pages(), writes=HS)
                    sc.op('dve', lambda e, n=n: e.tensor_tensor(out=rec.ap[:, n, :], in0=th.ap, in1=gel.ap[:, n, :],
                                                                op=ALU.mult),
                          reads=th.pages() + gel.cp(n), writes=rec.cp(n))
                sc.op('dve', lambda e, l=l: e.tensor_copy(
                    out=chist[:, l * 24:(l + 1) * 24].rearrange("p (a b) -> p a b", a=8), in_=zlx.ap[:, :, 512:515]),
                    reads=zlx.pages(), writes=CH)

                for cg in range(4):
                    c0 = cg * 512

                    def ev_q(b0, cg=cg):
                        for n in range(4):
                            ch = cg * 4 + n
                            copy_op(evac_eng(), qT.ap[:, ch, :], ps[:, b0 + n, :], reads=PSB(b0 + n),
                                    writes=qT.cp(ch), scale=128.0 ** -0.5)
                    gemm(lambda kb, c0=c0, l=l: w_in_d[l, kb * 1024:(kb + 1) * 1024, c0:c0 + 512], 4,
                         hT_chunk, hT_pages, ev_q)
                for cg in range(4):
                    c0 = AW + cg * 512

                    def ev_k(b0, cg=cg):
                        for n in range(4):
                            ch = cg * 4 + n
                            copy_op(evac_eng(), kst.ap[:, ch, :], ps[:, b0 + n, :], reads=PSB(b0 + n),
                                    writes=kst.cp(ch))
                    gemm(lambda kb, c0=c0, l=l: w_in_d[l, kb * 1024:(kb + 1) * 1024, c0:c0 + 512], 4,
                         hT_chunk, hT_pages, ev_k)
                KD = [('kT', l)]
                VD = [('vC', l)]
                sc.op('sp', lambda e, l=l, t0=t0: e.dma_start(
                    out=kT_d[l, :, t0:t0 + T].rearrange("(c p) t -> p c t", p=128), in_=kst.ap),
                    reads=kst.pages(), writes=KD)
                for cg in range(4):
                    c0 = 2 * AW + cg * 512

                    def ev_v(b0, cg=cg):
                        for tb in range(4):
                            copy_op(evac_eng(), vst.ap[:, tb, cg * 512:(cg + 1) * 512], ps[:, b0 + tb, :],
                                    reads=PSB(b0 + tb), writes=vst.pages(tb * 4096 + cg * 1024, tb * 4096 + (cg + 1) * 1024))
                    gemm(lambda kb, c0=c0, l=l: w_in_d[l, kb * 1024:(kb + 1) * 1024, c0:c0 + 512], 4,
                         hT_chunk, hT_pages, ev_v, mode='A')
                sc.op('sp', lambda e, l=l, t0=t0: e.dma_start(
                    out=vC_d[l, t0:t0 + T, :].rearrange("(tb p) f -> p tb f", p=128), in_=vst.ap),
                    reads=vst.pages(), writes=VD)

                nkb = 4 * (j + 1)
                nk_t = nkb * 128
                for h in range(8):
                    kb_ = kbuf[h % 2]
                    vb_ = vbuf[h % 2]
                    sc.op('sp', lambda e, l=l, h=h, kb_=kb_, nk_t=nk_t: e.dma_start(
                        out=kb_.ap[:, :, 0:nk_t],
                        in_=kT_d[l, h * 256:(h + 1) * 256, 0:nk_t].rearrange("(c p) t -> p c t", p=128)),
                        reads=KD, writes=kb_.pages())
                    sc.op('sp', lambda e, l=l, h=h, vb_=vb_, nk_t=nk_t, nkb=nkb: e.dma_start(
                        out=vb_.ap[:, 0:nkb, :],
                        in_=vC_d[l, 0:nk_t, h * 256:(h + 1) * 256].rearrange("(kb p) f -> p kb f", p=128)),
                        reads=VD, writes=vb_.pages())
                    for c in range(2):
                        ob = 2 + 3 * c
                        qch = 2 * h + c

                        def s_mm(kk, qch=qch, c=c, kb_=kb_):
                            m = max(0, kk - 4 * j)
                            lo = m * 128
                            sb = kk % 2
                            sc.op('pe', lambda e: e.matmul(ps[:, sb, lo:512], kb_.ap[:, c, kk * 128:(kk + 1) * 128],
                                                           qT.ap[:, qch, lo:512], start=True, stop=True),
                                  reads=kb_.pages() + qT.cp(qch), writes=PSB(sb))
                        s_mm(0)
                        for kk in range(nkb):
                            if kk + 1 < nkb:
                                s_mm(kk + 1)
                            m = max(0, kk - 4 * j)
                            lo = m * 128
                            sb = kk % 2
                            pt = pT[kk % 4]
                            sc.op('act', lambda e, sb=sb, lo=lo, pt=pt: e.activation(
                                out=pt.ap[:, lo:512], in_=ps[:, sb, lo:512], func=AF.Exp),
                                reads=PSB(sb), writes=pt.pages())
                            if kk >= 4 * j:
                                sc.op('dve', lambda e, lo=lo, pt=pt: e.tensor_tensor(
                                    out=pt.ap[:, lo:lo + 128], in0=pt.ap[:, lo:lo + 128], in1=tri_bf, op=ALU.mult),
                                    reads=pt.pages() + CBF, writes=pt.pages())

                            def pv(e, kk=kk, lo=lo, pt=pt, ob=ob, vb_=vb_, nkb=nkb):
                                e.matmul(ps[:, ob, lo:512], vb_.ap[:, kk, 0:128], pt.ap[:, lo:512],
                                         start=(kk == 0), stop=(kk == nkb - 1))
                                e.matmul(ps[:, ob + 1, lo:512], vb_.ap[:, kk, 128:256], pt.ap[:, lo:512],
                                         start=(kk == 0), stop=(kk == nkb - 1))
                                return e.matmul(ps[:, ob + 2, lo:512], ones_bf, pt.ap[:, lo:512],
                                                start=(kk == 0), stop=(kk == nkb - 1))
                            sc.op('pe', pv, reads=pt.pages() + vb_.pages() + CBF, writes=PSB(ob, 3))
                    LT = [('lamt',)]
                    sc.op('dve', lambda e: e.reciprocal(out=r1.ap, in_=ps[:, 4, :]), reads=PSB(4), writes=r1.pages())
                    sc.op('dve', lambda e: e.reciprocal(out=r2.ap, in_=ps[:, 7, :]), reads=PSB(7), writes=r2.pages())
                    sc.op('dve', lambda e, l=l: e.tensor_scalar(out=r2.ap, in0=r2.ap, scalar1=lamt[:, l * 4 + 3:l * 4 + 4],
                                                                scalar2=None, op0=ALU.mult),
                          reads=r2.pages() + LT, writes=r2.pages())
                    for ec in range(2):
                        sc.op('dve', lambda e, ec=ec: e.tensor_tensor(out=att.ap[:, ec, :], in0=ps[:, 2 + ec, :],
                                                                      in1=r1.ap, op=ALU.mult),
                              reads=PSB(2 + ec) + r1.pages(), writes=att.cp(ec))
                        sc.op('dve', lambda e, ec=ec: e.tensor_tensor(out=t1b.ap[:, ec, :], in0=ps[:, 5 + ec, :],
                                                                      in1=r2.ap, op=ALU.mult),
                              reads=PSB(5 + ec) + r2.pages(), writes=t1b.cp(ec))
                        sc.op('dve', lambda e, ec=ec: e.tensor_tensor(out=att.ap[:, ec, :], in0=att.ap[:, ec, :],
                                                                      in1=t1b.ap[:, ec, :], op=ALU.add),
                              reads=att.cp(ec) + t1b.cp(ec), writes=att.cp(ec))
                    for ec in range(2):
                        s = sq[ec]
                        sc.op('act', lambda e, ec=ec, s=s: e.activation(out=s.ap, in_=att.ap[:, ec, :], func=AF.Square),
                              reads=att.cp(ec), writes=s.pages())
                        sc.op('pe', lambda e, ec=ec, s=s: e.matmul(ps[:, 0, :], onesf, s.ap, start=(ec == 0),
                                                                   stop=(ec == 1)),
                              reads=s.pages() + CONST, writes=PSB(0))
                    sc.op('act', lambda e: e.activation(out=rstd.ap, in_=ps[:, 0, :], func=AF.Sqrt, bias=1.0,
                                                        scale=1.0 / (256 * EPS)), reads=PSB(0), writes=rstd.pages())
                    sc.op('dve', lambda e: e.reciprocal(out=rstd.ap, in_=rstd.ap), reads=rstd.pages(),
                          writes=rstd.pages())
                    for ec in range(2):
                        sc.op('dve', lambda e, ec=ec, h=h, b=b: e.scalar_tensor_tensor(
                            out=qT.ap[:, 2 * h + ec, :], in0=att.ap[:, ec, :], scalar=pp[:, b + 192 + ec:b + 193 + ec],
                            in1=rstd.ap, op0=ALU.mult, op1=ALU.mult),
                            reads=att.cp(ec) + rstd.pages() + PP, writes=qT.cp(2 * h + ec))
                state['psg'] = 0

                for cg in range(8):
                    cga = 3 * AW + 2 * RW + cg * 512
                    cgr = cga + D

                    def ev_ga(b0, cg=cg, gi=0, b=b):
                        for n in range(4):
                            blk = cg * 4 + n
                            sc.op('act', lambda e, n=n, blk=blk: e.activation(
                                out=sgate.ap[:, n, :], in_=ps[:, b0 + n, :], func=AF.Sigmoid,
                                bias=pp[:, b + 64 + gi * 32 + blk:b + 65 + gi * 32 + blk], scale=1.0),
                                reads=PSB(b0 + n) + PP, writes=sgate.cp(n))
                    gemm(lambda kb, c0=cga, l=l: w_in_d[l, kb * 1024:(kb + 1) * 1024, c0:c0 + 512], 4,
                         hT_chunk, hT_pages, ev_ga)

                    def ev_ya(b0):
                        for n in range(4):
                            sc.op('dve', lambda e, n=n: e.tensor_tensor(out=mtmp.ap[:, n, :], in0=ps[:, b0 + n, :],
                                                                        in1=sgate.ap[:, n, :], op=ALU.mult),
                                  reads=PSB(b0 + n) + sgate.cp(n), writes=mtmp.cp(n))
                    gemm(lambda kb, cg=cg, l=l: w_ap_d[l, kb * 1024:(kb + 1) * 1024, cg * 512:(cg + 1) * 512], 2,
                         lambda k: qT.ap[:, k, :], lambda kb: qT.cp(kb * 8, 8), ev_ya)
                    gemm(lambda kb, c0=cgr, l=l: w_in_d[l, kb * 1024:(kb + 1) * 1024, c0:c0 + 512], 4,
                         hT_chunk, hT_pages, lambda b0, cg=cg: ev_ga(b0, cg, 1))

                    def ev_yr(b0, cg=cg):
                        for n in range(4):
                            blk = cg * 4 + n
                            sc.op('dve', lambda e, n=n: e.tensor_tensor(
                                out=t1b.ap[:, n % 2, :], in0=ps[:, b0 + n, :], in1=sgate.ap[:, n, :], op=ALU.mult),
                                reads=PSB(b0 + n) + sgate.cp(n), writes=t1b.cp(n % 2))
                            sc.op('dve', lambda e, n=n, blk=blk: e.tensor_tensor(
                                out=mT.ap[:, blk, :], in0=t1b.ap[:, n % 2, :], in1=mtmp.ap[:, n, :], op=ALU.add),
                                reads=t1b.cp(n % 2) + mtmp.cp(n), writes=mT.cp(blk))
                    gemm(lambda kb, cg=cg, l=l: w_rp_d[l, kb * 1024:(kb + 1) * 1024, cg * 512:(cg + 1) * 512], 1,
                         lambda k: rec.ap[:, k, :], lambda kb: rec.pages(), ev_yr)

                def ev_res(b0, cg):
                    for n in range(4):
                        blk = cg * 4 + n
                        sc.op('dve', lambda e, n=n, blk=blk: e.tensor_tensor(
                            out=xacc.ap[:, blk, :], in0=ps[:, b0 + n, :], in1=xacc.ap[:, blk, :], op=ALU.add),
                            reads=PSB(b0 + n) + xacc.cp(blk), writes=xacc.cp(blk))
                for cg in range(8):
                    gemm(lambda kb, cg=cg, l=l: w_out_d[l, kb * 1024:(kb + 1) * 1024, cg * 512:(cg + 1) * 512], 4,
                         mT_chunk, mT_pages, lambda b0, cg=cg: ev_res(b0, cg))

                rmsnorm_to_hT(b + 32)
                for ffg in range(4):
                    for cg in range(8):
                        c0 = ffg * 4096 + cg * 512

                        def ev_up(b0, cg=cg):
                            for n in range(4):
                                blk = cg * 4 + n
                                r = rl[n % 2]
                                sc.op('act', lambda e, n=n, r=r: e.activation(out=r.ap, in_=ps[:, b0 + n, :], func=AF.Relu),
                                      reads=PSB(b0 + n), writes=r.pages())
                                sc.op('dve', lambda e, r=r, blk=blk: e.tensor_tensor(out=mT.ap[:, blk, :], in0=r.ap,
                                                                                     in1=r.ap, op=ALU.mult),
                                      reads=r.pages(), writes=mT.cp(blk))
                        gemm(lambda kb, c0=c0, l=l: w_up_d[l, kb * 1024:(kb + 1) * 1024, c0:c0 + 512], 4,
                             hT_chunk, hT_pages, ev_up)
                    for cg in range(8):
                        r0 = ffg * 4096
                        gemm(lambda kb, cg=cg, l=l, r0=r0: w_dn_d[l, r0 + kb * 1024:r0 + (kb + 1) * 1024,
                                                                  cg * 512:(cg + 1) * 512], 4,
                             mT_chunk, mT_pages, lambda b0, cg=cg: ev_res(b0, cg))

            bank = 4 * state['psg']
            state['psg'] ^= 1
            for c in range(NCH):
                s = sq[c % 2]
                sc.op('act', lambda e, c=c, s=s: e.activation(out=s.ap, in_=xacc.ap[:, c, :], func=AF.Square),
                      reads=xacc.cp(c), writes=s.pages())
                sc.op('pe', lambda e, c=c, s=s, bank=bank: e.matmul(ps[:, bank, :], onesf, s.ap, start=(c == 0),
                                                                    stop=(c == NCH - 1)),
                      reads=s.pages() + CONST, writes=PSB(bank))
            sc.op('act', lambda e, bank=bank: e.activation(out=rstd.ap, in_=ps[:, bank, :], func=AF.Sqrt, bias=1.0,
                                                           scale=1.0 / (D * EPS)), reads=PSB(bank), writes=rstd.pages())
            sc.op('dve', lambda e: e.reciprocal(out=rstd.ap, in_=rstd.ap), reads=rstd.pages(), writes=rstd.pages())
            for c in range(NCH):
                sc.op('dve', lambda e, c=c: e.scalar_tensor_tensor(
                    out=xacc.ap[:, c, :], in0=xacc.ap[:, c, :], scalar=pp[:, fb + c:fb + c + 1], in1=rstd.ap,
                    op0=ALU.mult, op1=ALU.mult), reads=xacc.cp(c) + rstd.pages() + PP, writes=xacc.cp(c))
            for tb in range(4):
                xs = xstage[tb % 2]
                for c4 in range(8):
                    bank = (c4 % 2) * 4 + (c4 // 2) % 4

                    def tr2(e, c4=c4, tb=tb, bank=bank):
                        last = None
                        for q in range(4):
                            last = e.transpose(ps[:, bank, q * 128:(q + 1) * 128],
                                               xacc.ap[:, c4 * 4 + q, tb * 128:(tb + 1) * 128], identf)
                        return last
                    sc.op('pe', tr2, reads=xacc.cp(c4 * 4, 4) + CONST, writes=PSB(bank))
                    copy_op(evac_eng(), xs.ap[:, c4 * 4:(c4 + 1) * 4, :],
                            ps[:, bank, :].rearrange("p (a b) -> p a b", a=4),
                            reads=PSB(bank), writes=xs.pages(c4 * 2048, (c4 + 1) * 2048))
                r0 = t0 + tb * 128
                sc.op('sp', lambda e, xs=xs, r0=r0: e.dma_start(
                    out=out_d[r0:r0 + 128, :], in_=xs.ap.rearrange("p a b -> p (a b)")),
                    reads=xs.pages(), writes=[('out', r0)])

        sc.emit(block)
    return nc


def pack_params(inp, L):
    def pc(v):
        v = np.asarray(v, dtype=np.float32)
        return v.reshape(-1, 128).T
    cols = []
    for l in range(4):
        if l < L:
            cols += [pc(inp["norm1_g"][l]), pc(inp["norm2_g"][l]), pc(inp["gate_b"][l, 0]), pc(inp["gate_b"][l, 1])]
            cols += [pc(inp["conv_w"][l, jj]) for jj in range(4)]
            cols += [pc(inp["conv_b"][l]), pc(inp["b_rgate"][l]), pc(inp["b_igate"][l]), pc(inp["lru_lambda"][l]),
                     pc(inp["subln_g"][l])]
        else:
            cols.append(np.ones((128, PPL), np.float32))
    cols.append(pc(inp["final_g"]))
    return np.ascontiguousarray(np.concatenate(cols, axis=1))


def make_consts():
    c = np.zeros((128, 384), np.float32)
    c[:, 0:128] = np.eye(128, dtype=np.float32)
    c[:, 128:256] = np.triu(np.ones((128, 128), np.float32))
    c[:, 256:384] = 1.0
    return c


def kernel(**inputs):
    inp = {k: np.asarray(v) for k, v in inputs.items()}
    L, NT, NCORES = 4, 4, 8
    nc = build(L, NT)
    pp = pack_params(inp, L)
    consts = make_consts()
    shared = {
        "w_in": inp["w_in"], "w_attn_proj": inp["w_attn_proj"], "w_rec_proj": inp["w_rec_proj"],
        "w_out": inp["w_out"], "w_up": inp["w_up"], "w_down": inp["w_down"],
        "w_rgate": inp["w_rgate"], "w_igate": inp["w_igate"], "pp": pp,
        "lam_qk": np.ascontiguousarray(inp["lam_qk"], dtype=np.float32), "consts": consts,
    }
    in_maps = []
    for c in range(NCORES):
        m = dict(shared)
        m["x"] = np.ascontiguousarray(inp["x"][c], dtype=np.float32)
        in_maps.append(m)
    res = run_bass_kernel_spmd(nc, in_maps, core_ids=list(range(NCORES)))
    return np.stack([np.asarray(r["out"]) for r in res.results], axis=0).astype(np.float32)
```

```python
import math
from contextlib import ExitStack
import numpy as np
import concourse.bass as bass
import concourse.mybir as mybir
from concourse.bass_utils import run_bass_kernel_spmd

F32 = mybir.dt.float32
BF16 = mybir.dt.bfloat16
U8 = mybir.dt.uint8
AF = mybir.ActivationFunctionType
ALU = mybir.AluOpType
AX = mybir.AxisListType

S = 2048
D = 4096
T = 512
AW = 2048
RW = 1024
FF = 16384
CIN = 16384
EPS = 1e-6
NCH = D // 128
PPL = 194
KB = 1024


class Sched:
    def __init__(self, nc, stack):
        self.nc = nc
        self.engs = ['pe', 'act', 'dve', 'sp', 'gq']
        self.ops = {e: [] for e in self.engs}
        self.cnt = {e: 0 for e in self.engs}
        self.sem = {e: stack.enter_context(nc.semaphore('s_' + e)) for e in ['pe', 'act', 'dve']}
        self.NDS = 16
        self.dsem = {q: [stack.enter_context(nc.semaphore('d_%s%d' % (q, i))) for i in range(self.NDS)]
                     for q in ['sp', 'gq']}
        self.dval = {q: [0] * self.NDS for q in ['sp', 'gq']}
        self.dnext = {q: 0 for q in ['sp', 'gq']}
        self.waited = {e: {} for e in self.engs}
        self.lastw = {}
        self.readers = {}

    def op(self, eng, fn, reads=(), writes=()):
        deps = {}

        def add(tok):
            if tok is None:
                return
            teng, sem, val = tok
            if teng == 'pe' and eng == 'pe':
                return
            key = id(sem)
            if key not in deps or deps[key][1] < val:
                deps[key] = (sem, val)
        for p in reads:
            add(self.lastw.get(p))
        for p in writes:
            add(self.lastw.get(p))
            for t in self.readers.get(p, {}).values():
                add(t)
        if eng in ('sp', 'gq'):
            k = self.dnext[eng]
            self.dnext[eng] = (k + 1) % self.NDS
            sem = self.dsem[eng][k]
            prev = self.dval[eng][k]
            if prev > 0:
                add(('dma_prev', sem, prev))
            self.dval[eng][k] = prev + 16
            tok = (eng, sem, prev + 16)
            inc = 16
        else:
            self.cnt[eng] += 1
            sem = self.sem[eng]
            tok = (eng, sem, self.cnt[eng])
            inc = 1
        waits = []
        w = self.waited[eng]
        for key, (s, v) in deps.items():
            if w.get(key, 0) >= v:
                continue
            w[key] = v
            waits.append((s, v))
        self.ops[eng].append((waits, fn, sem, inc))
        for p in writes:
            self.lastw[p] = tok
            self.readers[p] = {}
        for p in reads:
            self.readers.setdefault(p, {})[id(tok[1])] = tok
        return tok

    def emit(self, block):
        def mk(name):
            def body(e):
                for waits, fn, sem, inc in self.ops[name]:
                    for s, v in waits:
                        e.wait_ge(s, v)
                    ins = fn(e)
                    ins.then_inc(sem, inc)
                if name == 'sp':
                    for q in ('sp', 'gq'):
                        for k in range(self.NDS):
                            if self.dval[q][k] > 0:
                                e.wait_ge(self.dsem[q][k], self.dval[q][k])
            return body
        block.tensor(mk('pe'))
        block.scalar(mk('act'))
        block.vector(mk('dve'))
        block.sync(mk('sp'))
        block.gpsimd(mk('gq'))


class Buf:
    def __init__(self, arena, off, shape, dtype):
        esz = 4 if dtype == F32 else 2
        n = 1
        for s in shape:
            n *= s
        self.off = off
        self.nbytes = n * esz
        ap = arena[:, off:off + n * esz].bitcast(dtype)
        if len(shape) == 2:
            ap = ap.rearrange("p (a b) -> p a b", a=shape[0])
        elif len(shape) == 3:
            ap = ap.rearrange("p (a b c) -> p a b c", a=shape[0], b=shape[1])
        self.ap = ap
        self.shape = shape

    def pages(self, lo=None, hi=None):
        lo = 0 if lo is None else lo
        hi = self.nbytes if hi is None else hi
        return [('a', i) for i in range((self.off + lo) // KB, (self.off + hi - 1) // KB + 1)]

    def cp(self, c, n=1):
        per = self.nbytes // self.shape[0]
        return self.pages(c * per, (c + n) * per)


def build(L, NT, n_in_layers=None):
    nc = bass.Bass("TRN2", target_bir_lowering=False)
    LW = L if n_in_layers is None else n_in_layers
    x_d = nc.dram_tensor("x", [S, D], F32, kind="ExternalInput").ap()
    w_in_d = nc.dram_tensor("w_in", [LW, D, CIN], F32, kind="ExternalInput").ap()
    w_ap_d = nc.dram_tensor("w_attn_proj", [LW, AW, D], F32, kind="ExternalInput").ap()
    w_rp_d = nc.dram_tensor("w_rec_proj", [LW, RW, D], F32, kind="ExternalInput").ap()
    w_out_d = nc.dram_tensor("w_out", [LW, D, D], F32, kind="ExternalInput").ap()
    w_up_d = nc.dram_tensor("w_up", [LW, D, FF], F32, kind="ExternalInput").ap()
    w_dn_d = nc.dram_tensor("w_down", [LW, FF, D], F32, kind="ExternalInput").ap()
    w_rg_d = nc.dram_tensor("w_rgate", [LW, 8, 128, 128], F32, kind="ExternalInput").ap()
    w_ig_d = nc.dram_tensor("w_igate", [LW, 8, 128, 128], F32, kind="ExternalInput").ap()
    pp_d = nc.dram_tensor("pp", [128, PPL * 4 + 32], F32, kind="ExternalInput").ap()
    lamqk_d = nc.dram_tensor("lam_qk", [4, 4, 128], F32, kind="ExternalInput").ap()
    consts_d = nc.dram_tensor("consts", [128, 384], F32, kind="ExternalInput").ap()
    out_d = nc.dram_tensor("out", [S, D], F32, kind="ExternalOutput").ap()
    kT_d = nc.dram_tensor("kT_cache", [L, AW, S], BF16).ap()
    vC_d = nc.dram_tensor("v_cache", [L, S, AW], BF16).ap()

    with ExitStack() as st:
        ARENA = 198 * KB
        arena = st.enter_context(nc.sbuf_tensor("arena", [128, ARENA], U8))
        ident = st.enter_context(nc.sbuf_tensor("ident", [128, 384], F32))
        cbf = st.enter_context(nc.sbuf_tensor("cbf", [128, 256], BF16))
        pp = st.enter_context(nc.sbuf_tensor("ppt", [128, PPL * 4 + 32], F32))
        lamt = st.enter_context(nc.sbuf_tensor("lamt", [128, 16], F32))
        hst = st.enter_context(nc.sbuf_tensor("hst", [128, 4 * 8], F32))
        chist = st.enter_context(nc.sbuf_tensor("chist", [128, 4 * 8 * 3], F32))
        ps = st.enter_context(nc.psum_tensor("ps", [128, 8, 512], F32))
        sc = Sched(nc, st)
        block = st.enter_context(nc.Block())

        identf = ident[:, 0:128]
        onesf = ident[:, 256:384]
        tri_bf = cbf[:, 0:128]
        ones_bf = cbf[:, 128:256]

        xacc = Buf(arena, 0, (32, 512), F32)
        hT = Buf(arena, 64 * KB, (32, 512), BF16)
        mT = Buf(arena, 96 * KB, (32, 512), BF16)
        kbuf = [Buf(arena, 96 * KB + s * 8 * KB, (2, 2048), BF16) for s in range(2)]
        vbuf = [Buf(arena, 112 * KB + s * 8 * KB, (16, 256), BF16) for s in range(2)]
        xstage = [Buf(arena, 96 * KB + s * 16 * KB, (32, 128), F32) for s in range(2)]
        vst = Buf(arena, 96 * KB, (4, 2048), BF16)
        wg = Buf(arena, 96 * KB, (2, 8, 128), BF16)
        gel = Buf(arena, 104 * KB, (8, 512), BF16)
        ltmp = [Buf(arena, 112 * KB + i * 2 * KB, (512,), F32) for i in range(8)]
        qT = Buf(arena, 128 * KB, (16, 512), BF16)
        NWB = 3
        wb = [Buf(arena, 144 * KB + s * 8 * KB, (8, 512), BF16) for s in range(NWB)]
        rec = Buf(arena, 168 * KB, (8, 512), BF16)
        zlx = Buf(arena, 176 * KB, (8, 515), F32)
        kst = Buf(arena, 176 * KB, (16, 512), BF16)
        pT = [Buf(arena, 176 * KB + i * KB, (512,), BF16) for i in range(4)]
        r1 = Buf(arena, 180 * KB, (512,), F32)
        r2 = Buf(arena, 182 * KB, (512,), F32)
        att = Buf(arena, 184 * KB, (2, 512), F32)
        t1b = Buf(arena, 188 * KB, (2, 512), F32)
        sgate = Buf(arena, 176 * KB, (4, 512), BF16)
        mtmp = Buf(arena, 180 * KB, (4, 512), F32)
        rl = [Buf(arena, 176 * KB + i * 2 * KB, (512,), F32) for i in range(2)]
        sq = [Buf(arena, 192 * KB + i * 2 * KB, (512,), F32) for i in range(2)]
        rstd = Buf(arena, 196 * KB, (512,), F32)
        lqb = Buf(arena, 0, (2048,), F32)

        PSB = lambda b, n=1: [('ps', i) for i in range(b, b + n)]
        PP = [('pp',)]
        CONST = [('const',)]

        sc.op('sp', lambda e: e.dma_start(out=ident[:], in_=consts_d[:, :]), writes=CONST)
        sc.op('sp', lambda e: e.dma_start(out=pp[:], in_=pp_d[:, :]), writes=PP)
        sc.op('sp', lambda e: e.dma_start(
            out=lqb.ap, in_=lamqk_d.rearrange("l f d -> (l f d)").partition_broadcast(128)),
            writes=lqb.pages())
        sc.op('dve', lambda e: e.tensor_copy(out=cbf[:, 0:128], in_=ident[:, 128:256]), reads=CONST, writes=[('cbf',)])
        sc.op('dve', lambda e: e.tensor_copy(out=cbf[:, 128:256], in_=ident[:, 256:384]), reads=CONST, writes=[('cbf',)])
        CBF = [('cbf',)]
        sc.op('dve', lambda e: e.memset(hst[:], 0.0), writes=[('hst',)])
        sc.op('dve', lambda e: e.memset(chist[:], 0.0), writes=[('chist',)])
        for l in range(L):
            b = l * PPL
            lam_init = 0.8 - 0.6 * math.exp(-0.3 * l)
            sc.op('dve', lambda e, b=b: e.tensor_scalar(out=pp[:, b:b + 64], in0=pp[:, b:b + 64], scalar1=EPS ** -0.5,
                                                        scalar2=None, op0=ALU.mult), reads=PP, writes=PP)
            sc.op('act', lambda e, b=b: e.activation(out=pp[:, b + 184:b + 192], in_=pp[:, b + 184:b + 192],
                                                     func=AF.Exp, scale=-1.0), reads=PP, writes=PP)
            sc.op('act', lambda e, b=b: e.activation(out=pp[:, b + 184:b + 192], in_=pp[:, b + 184:b + 192],
                                                     func=AF.Ln, bias=1.0, scale=1.0), reads=PP, writes=PP)
            sc.op('dve', lambda e, b=b: e.tensor_scalar(out=pp[:, b + 184:b + 192], in0=pp[:, b + 184:b + 192],
                                                        scalar1=-8.0, scalar2=None, op0=ALU.mult), reads=PP, writes=PP)
            sc.op('dve', lambda e, b=b, c=(EPS ** -0.5) * (1.0 - lam_init): e.tensor_scalar(
                out=pp[:, b + 192:b + 194], in0=pp[:, b + 192:b + 194], scalar1=c, scalar2=None, op0=ALU.mult),
                reads=PP, writes=PP)
            LT = [('lamt',)]
            for hf in range(2):
                o0 = l * 512 + hf * 256
                sc.op('dve', lambda e, o0=o0: e.tensor_tensor(out=lqb.ap[:, o0:o0 + 128], in0=lqb.ap[:, o0:o0 + 128],
                                                              in1=lqb.ap[:, o0 + 128:o0 + 256], op=ALU.mult),
                      reads=lqb.pages(), writes=lqb.pages())
                sc.op('dve', lambda e, o0=o0, l=l, hf=hf: e.reduce_sum(out=lamt[:, l * 4 + hf:l * 4 + hf + 1],
                                                                      in_=lqb.ap[:, o0:o0 + 128], axis=AX.X),
                      reads=lqb.pages(), writes=LT)
            sc.op('act', lambda e, l=l: e.activation(out=lamt[:, l * 4:l * 4 + 2], in_=lamt[:, l * 4:l * 4 + 2],
                                                     func=AF.Exp), reads=LT, writes=LT)
            sc.op('dve', lambda e, l=l: e.tensor_tensor(out=lamt[:, l * 4 + 2:l * 4 + 3], in0=lamt[:, l * 4:l * 4 + 1],
                                                        in1=lamt[:, l * 4 + 1:l * 4 + 2], op=ALU.subtract),
                  reads=LT, writes=LT)
            sc.op('dve', lambda e, l=l, li=lam_init: e.tensor_scalar(
                out=lamt[:, l * 4 + 3:l * 4 + 4], in0=lamt[:, l * 4 + 2:l * 4 + 3], scalar1=li, scalar2=-1.0,
                op0=ALU.add, op1=ALU.mult), reads=LT, writes=LT)
        fb = 4 * PPL
        sc.op('dve', lambda e: e.tensor_scalar(out=pp[:, fb:fb + 32], in0=pp[:, fb:fb + 32], scalar1=EPS ** -0.5,
                                               scalar2=None, op0=ALU.mult), reads=PP, writes=PP)

        state = {'wslot': 0, 'psg': 0, 'cp': 0}

        def evac_eng():
            state['cp'] ^= 1
            return 'act' if state['cp'] else 'dve'

        def copy_op(eng, out, in_, reads, writes, scale=None):
            if eng == 'act':
                if scale is None:
                    sc.op('act', lambda e: e.activation(out=out, in_=in_, func=AF.Copy), reads=reads, writes=writes)
                else:
                    sc.op('act', lambda e: e.activation(out=out, in_=in_, func=AF.Copy, scale=scale),
                          reads=reads, writes=writes)
            else:
                if scale is None:
                    sc.op('dve', lambda e: e.tensor_copy(out=out, in_=in_), reads=reads, writes=writes)
                else:
                    sc.op('dve', lambda e: e.tensor_scalar(out=out, in0=in_, scalar1=scale, scalar2=None,
                                                           op0=ALU.mult), reads=reads, writes=writes)

        def gemm(wsrc, nk, act_chunk, act_pages, evac, mode='B', defer=False):
            g = state['psg']
            state['psg'] ^= 1
            b0 = 4 * g
            for kb in range(nk):
                s = state['wslot']
                state['wslot'] = (s + 1) % NWB
                src = wsrc(kb).rearrange("(kc p) c -> p kc c", p=128)
                sc.op('gq', lambda e, s=s, src=src: e.dma_start(out=wb[s].ap, in_=src), writes=wb[s].pages())

                def mm(e, s=s, kb=kb):
                    last = None
                    for kc in range(8):
                        k = kb * 8 + kc
                        for n in range(4):
                            if mode == 'B':
                                last = e.matmul(ps[:, b0 + n, :], wb[s].ap[:, kc, n * 128:(n + 1) * 128],
                                                act_chunk(k), start=(k == 0), stop=(k == nk * 8 - 1))
                            else:
                                last = e.matmul(ps[:, b0 + n, :], act_chunk(k)[:, n * 128:(n + 1) * 128],
                                                wb[s].ap[:, kc, :], start=(k == 0), stop=(k == nk * 8 - 1))
                    return last
                sc.op('pe', mm, reads=wb[s].pages() + act_pages(kb), writes=PSB(b0, 4))
            if defer:
                return lambda: evac(b0)
            evac(b0)

        def rmsnorm_to_hT(gcol):
            bank = 4 * state['psg']
            state['psg'] ^= 1
            for c in range(NCH):
                s = sq[c % 2]
                if c % 2 == 0:
                    sc.op('act', lambda e, c=c, s=s: e.activation(out=s.ap, in_=xacc.ap[:, c, :], func=AF.Square),
                          reads=xacc.cp(c), writes=s.pages())
                else:
                    sc.op('dve', lambda e, c=c, s=s: e.tensor_tensor(out=s.ap, in0=xacc.ap[:, c, :],
                                                                     in1=xacc.ap[:, c, :], op=ALU.mult),
                          reads=xacc.cp(c), writes=s.pages())
                sc.op('pe', lambda e, c=c, s=s: e.matmul(ps[:, bank, :], onesf, s.ap, start=(c == 0),
                                                         stop=(c == NCH - 1)),
                      reads=s.pages() + CONST, writes=PSB(bank))
            sc.op('act', lambda e: e.activation(out=rstd.ap, in_=ps[:, bank, :], func=AF.Sqrt, bias=1.0,
                                                scale=1.0 / (D * EPS)), reads=PSB(bank), writes=rstd.pages())
            sc.op('dve', lambda e: e.reciprocal(out=rstd.ap, in_=rstd.ap), reads=rstd.pages(), writes=rstd.pages())
            for c in range(NCH):
                sc.op('dve', lambda e, c=c: e.scalar_tensor_tensor(
                    out=hT.ap[:, c, :], in0=xacc.ap[:, c, :], scalar=pp[:, gcol + c:gcol + c + 1], in1=rstd.ap,
                    op0=ALU.mult, op1=ALU.mult), reads=xacc.cp(c) + rstd.pages() + PP, writes=hT.cp(c))

        hT_chunk = lambda k: hT.ap[:, k, :]
        hT_pages = lambda kb: hT.cp(kb * 8, 8)
        mT_chunk = lambda k: mT.ap[:, k, :]
        mT_pages = lambda kb: mT.cp(kb * 8, 8)

        for j in range(NT):
            t0 = j * T
            for tb in range(4):
                xs = xstage[tb % 2]
                r0 = t0 + tb * 128
                sc.op('sp', lambda e, xs=xs, r0=r0: e.dma_start(
                    out=xs.ap.rearrange("p a b -> p (a b)"), in_=x_d[r0:r0 + 128, :]), writes=xs.pages())
                for c4 in range(8):
                    bank = (c4 % 2) * 4 + (c4 // 2) % 4

                    def tr(e, xs=xs, c4=c4, bank=bank):
                        last = None
                        for q in range(4):
                            last = e.transpose(ps[:, bank, q * 128:(q + 1) * 128], xs.ap[:, c4 * 4 + q, :], identf)
                        return last
                    sc.op('pe', tr, reads=xs.pages() + CONST, writes=PSB(bank))
                    eng = evac_eng()
                    copy_op(eng, xacc.ap[:, c4 * 4:(c4 + 1) * 4, tb * 128:(tb + 1) * 128],
                            ps[:, bank, :].rearrange("p (a b) -> p a b", a=4),
                            reads=PSB(bank), writes=xacc.cp(c4 * 4, 4))

            for l in range(L):
                b = l * PPL
                rmsnorm_to_hT(b)

                sc.op('gq', lambda e, l=l: e.dma_start(out=wg.ap[:, 0], in_=w_rg_d[l].rearrange("n c d -> c n d")),
                      writes=wg.pages())
                sc.op('gq', lambda e, l=l: e.dma_start(out=wg.ap[:, 1], in_=w_ig_d[l].rearrange("n c d -> c n d")),
                      writes=wg.pages())
                CH = [('chist',)]
                HS = [('hst',)]
                sc.op('dve', lambda e, l=l: e.tensor_copy(
                    out=zlx.ap[:, :, 0:3], in_=chist[:, l * 24:(l + 1) * 24].rearrange("p (a b) -> p a b", a=8)),
                    reads=CH, writes=zlx.pages())
                for cg in range(2):
                    c0 = 3 * AW + cg * 512

                    def ev_lx(b0, cg=cg):
                        for n in range(4):
                            ch = cg * 4 + n
                            copy_op(evac_eng(), zlx.ap[:, ch, 3:515], ps[:, b0 + n, :], reads=PSB(b0 + n),
                                    writes=zlx.pages())
                    gemm(lambda kb, c0=c0, l=l: w_in_d[l, kb * 1024:(kb + 1) * 1024, c0:c0 + 512], 4,
                         hT_chunk, hT_pages, ev_lx)
                for cg in range(2):
                    c0 = 3 * AW + RW + cg * 512

                    def ev_ly(b0, cg=cg):
                        for n in range(4):
                            ch = cg * 4 + n
                            ta, tb_ = ltmp[(n % 2) * 2], ltmp[(n % 2) * 2 + 1]
                            P = PSB(b0 + n)
                            sc.op('act', lambda e, ta=ta, b0=b0, n=n: e.activation(
                                out=ta.ap, in_=ps[:, b0 + n, :], func=AF.Square), reads=P, writes=ta.pages())
                            sc.op('dve', lambda e, ta=ta: e.tensor_scalar(
                                out=ta.ap, in0=ta.ap, scalar1=0.044715, scalar2=1.0, op0=ALU.mult, op1=ALU.add),
                                reads=ta.pages(), writes=ta.pages())
                            sc.op('dve', lambda e, ta=ta, b0=b0, n=n: e.tensor_tensor(
                                out=ta.ap, in0=ta.ap, in1=ps[:, b0 + n, :], op=ALU.mult),
                                reads=ta.pages() + P, writes=ta.pages())
                            sc.op('act', lambda e, ta=ta, tb_=tb_: e.activation(
                                out=tb_.ap, in_=ta.ap, func=AF.Sigmoid, scale=1.5957691216057308),
                                reads=ta.pages(), writes=tb_.pages())
                            sc.op('dve', lambda e, tb_=tb_, b0=b0, n=n, ch=ch: e.tensor_tensor(
                                out=gel.ap[:, ch, :], in0=tb_.ap, in1=ps[:, b0 + n, :], op=ALU.mult),
                                reads=tb_.pages() + P, writes=gel.cp(ch))
                    gemm(lambda kb, c0=c0, l=l: w_in_d[l, kb * 1024:(kb + 1) * 1024, c0:c0 + 512], 4,
                         hT_chunk, hT_pages, ev_ly)
                def lru_chunk(n, bank, l=l, b=b):
                    xr, xrb, tr_, ti, ta, tm, th, tx = ltmp
                    cw = lambda jj, n=n, b=b: pp[:, b + 128 + jj * 8 + n:b + 128 + jj * 8 + n + 1]
                    col = lambda o, n=n, b=b: pp[:, b + o + n:b + o + n + 1]
                    ZP = zlx.pages()
                    sc.op('dve', lambda e, n=n, cw=cw, col=col: e.tensor_scalar(
                        out=xr.ap, in0=zlx.ap[:, n, 0:512], scalar1=cw(0), scalar2=col(160), op0=ALU.mult, op1=ALU.add),
                        reads=ZP + PP, writes=xr.pages())
                    for jj in range(1, 4):
                        sc.op('dve', lambda e, n=n, jj=jj, cw=cw: e.scalar_tensor_tensor(
                            out=xr.ap, in0=zlx.ap[:, n, jj:jj + 512], scalar=cw(jj), in1=xr.ap, op0=ALU.mult, op1=ALU.add),
                            reads=ZP + PP + xr.pages(), writes=xr.pages())
                    sc.op('act', lambda e: e.activation(out=xrb.ap[:, 0:256].bitcast(BF16), in_=xr.ap, func=AF.Copy),
                          reads=xr.pages(), writes=xrb.pages())
                    xrb16 = xrb.ap[:, 0:256].bitcast(BF16)
                    sc.op('pe', lambda e, n=n, bank=bank, xrb16=xrb16: e.matmul(
                        ps[:, bank, :], wg.ap[:, 0, n, :], xrb16, start=True, stop=True),
                        reads=wg.pages() + xrb.pages(), writes=PSB(bank))
                    sc.op('pe', lambda e, n=n, bank=bank, xrb16=xrb16: e.matmul(
                        ps[:, bank + 1, :], wg.ap[:, 1, n, :], xrb16, start=True, stop=True),
                        reads=wg.pages() + xrb.pages(), writes=PSB(bank + 1))
                    sc.op('act', lambda e, bank=bank, col=col: e.activation(
                        out=tr_.ap, in_=ps[:, bank, :], func=AF.Sigmoid, bias=col(168), scale=1.0),
                        reads=PSB(bank) + PP, writes=tr_.pages())
                    sc.op('act', lambda e, bank=bank, col=col: e.activation(
                        out=ti.ap, in_=ps[:, bank + 1, :], func=AF.Sigmoid, bias=col(176), scale=1.0),
                        reads=PSB(bank + 1) + PP, writes=ti.pages())
                    sc.op('act', lambda e, col=col: e.activation(out=ta.ap, in_=tr_.ap, func=AF.Exp, scale=col(184)),
                          reads=tr_.pages() + PP, writes=ta.pages())
                    sc.op('dve', lambda e: e.tensor_tensor(out=tm.ap, in0=ta.ap, in1=ta.ap, op=ALU.mult),
                          reads=ta.pages(), writes=tm.pages())
                    sc.op('act', lambda e: e.activation(out=tm.ap, in_=tm.ap, func=AF.Sqrt, bias=1.0, scale=-1.0),
                          reads=tm.pages(), writes=tm.pages())
                    sc.op('dve', lambda e: e.tensor_tensor(out=ti.ap, in0=ti.ap, in1=xr.ap, op=ALU.mult),
                          reads=ti.pages() + xr.pages(), writes=ti.pages())
                    sc.op('dve', lambda e: e.tensor_tensor(out=tm.ap, in0=tm.ap, in1=ti.ap, op=ALU.mult),
                          reads=ti.pages() + tm.pages(), writes=tm.pages())
                    sc.op('dve', lambda e, l=l, n=n: e.tensor_tensor_scan(
                        out=th.ap, data0=ta.ap, data1=tm.ap, initial=hst[:, l * 8 + n:l * 8 + n + 1],
                        op0=ALU.mult, op1=ALU.add), reads=ta.pages() + tm.pages() + HS, writes=th.pages())
                    sc.op('dve', lambda e, l=l, n=n: e.tensor_copy(out=hst[:, l * 8 + n:l * 8 + n + 1],
                                                                   in_=th.ap[:, 511:512]),
                          reads=th.pages(), writes=HS)
                    sc.op('dve', lambda e, n=n: e.tensor_tensor(out=rec.ap[:, n, :], in0=th.ap, in1=gel.ap[:, n, :],
                                                                op=ALU.mult),
                          reads=th.pages() + gel.cp(n), writes=rec.cp(n))
                for cg in range(4):
                    c0 = cg * 512

                    def ev_q(b0, cg=cg):
                        for n in range(4):
                            ch = cg * 4 + n
                            copy_op(evac_eng(), qT.ap[:, ch, :], ps[:, b0 + n, :], reads=PSB(b0 + n),
                                    writes=qT.cp(ch), scale=128.0 ** -0.5)
                    fin = gemm(lambda kb, c0=c0, l=l: w_in_d[l, kb * 1024:(kb + 1) * 1024, c0:c0 + 512], 4,
                               hT_chunk, hT_pages, ev_q, defer=True)
                    gb = 4 * state['psg']
                    lru_chunk(2 * cg, gb)
                    lru_chunk(2 * cg + 1, gb + 2)
                    fin()
                sc.op('dve', lambda e, l=l: e.tensor_copy(
                    out=chist[:, l * 24:(l + 1) * 24].rearrange("p (a b) -> p a b", a=8), in_=zlx.ap[:, :, 512:515]),
                    reads=zlx.pages(), writes=CH)

                for cg in range(4):
                    c0 = AW + cg * 512

                    def ev_k(b0, cg=cg):
                        for n in range(4):
                            ch = cg * 4 + n
                            copy_op(evac_eng(), kst.ap[:, ch, :], ps[:, b0 + n, :], reads=PSB(b0 + n),
                                    writes=kst.cp(ch))
                    gemm(lambda kb, c0=c0, l=l: w_in_d[l, kb * 1024:(kb + 1) * 1024, c0:c0 + 512], 4,
                         hT_chunk, hT_pages, ev_k)
                KD = [('kT', l)]
                VD = [('vC', l)]
                sc.op('sp', lambda e, l=l, t0=t0: e.dma_start(
                    out=kT_d[l, :, t0:t0 + T].rearrange("(c p) t -> p c t", p=128), in_=kst.ap),
                    reads=kst.pages(), writes=KD)
                for cg in range(4):
                    c0 = 2 * AW + cg * 512

                    def ev_v(b0, cg=cg):
                        for tb in range(4):
                            copy_op(evac_eng(), vst.ap[:, tb, cg * 512:(cg + 1) * 512], ps[:, b0 + tb, :],
                                    reads=PSB(b0 + tb), writes=vst.pages(tb * 4096 + cg * 1024, tb * 4096 + (cg + 1) * 1024))
                    gemm(lambda kb, c0=c0, l=l: w_in_d[l, kb * 1024:(kb + 1) * 1024, c0:c0 + 512], 4,
                         hT_chunk, hT_pages, ev_v, mode='A')
                sc.op('sp', lambda e, l=l, t0=t0: e.dma_start(
                    out=vC_d[l, t0:t0 + T, :].rearrange("(tb p) f -> p tb f", p=128), in_=vst.ap),
                    reads=vst.pages(), writes=VD)

                nkb = 4 * (j + 1)
                nk_t = nkb * 128
                def emit_subln(h, b=b):
                    for ec in range(2):
                        s_ = sq[ec]
                        sc.op('act', lambda e, ec=ec, s_=s_: e.activation(out=s_.ap, in_=att.ap[:, ec, :], func=AF.Square),
                              reads=att.cp(ec), writes=s_.pages())
                        sc.op('pe', lambda e, ec=ec, s_=s_: e.matmul(ps[:, 5, :], onesf, s_.ap, start=(ec == 0),
                                                                     stop=(ec == 1)),
                              reads=s_.pages() + CONST, writes=PSB(5))
                    sc.op('act', lambda e: e.activation(out=rstd.ap, in_=ps[:, 5, :], func=AF.Sqrt, bias=1.0,
                                                        scale=1.0 / (256 * EPS)), reads=PSB(5), writes=rstd.pages())
                    sc.op('dve', lambda e: e.reciprocal(out=rstd.ap, in_=rstd.ap), reads=rstd.pages(),
                          writes=rstd.pages())
                    for ec in range(2):
                        sc.op('dve', lambda e, ec=ec, h=h, b=b: e.scalar_tensor_tensor(
                            out=qT.ap[:, 2 * h + ec, :], in0=att.ap[:, ec, :], scalar=pp[:, b + 192 + ec:b + 193 + ec],
                            in1=rstd.ap, op0=ALU.mult, op1=ALU.mult),
                            reads=att.cp(ec) + rstd.pages() + PP, writes=qT.cp(2 * h + ec))
                for h in range(8):
                    kb_ = kbuf[h % 2]
                    vb_ = vbuf[h % 2]
                    sc.op('sp', lambda e, l=l, h=h, kb_=kb_, nk_t=nk_t: e.dma_start(
                        out=kb_.ap[:, :, 0:nk_t],
                        in_=kT_d[l, h * 256:(h + 1) * 256, 0:nk_t].rearrange("(c p) t -> p c t", p=128)),
                        reads=KD, writes=kb_.pages())
                    sc.op('sp', lambda e, l=l, h=h, vb_=vb_, nk_t=nk_t, nkb=nkb: e.dma_start(
                        out=vb_.ap[:, 0:nkb, :],
                        in_=vC_d[l, 0:nk_t, h * 256:(h + 1) * 256].rearrange("(kb p) f -> p kb f", p=128)),
                        reads=VD, writes=vb_.pages())
                    for c in range(2):
                        ob = 2 + 3 * c
                        qch = 2 * h + c

                        def s_mm(kk, qch=qch, c=c, kb_=kb_):
                            m = max(0, kk - 4 * j)
                            lo = m * 128
                            sb = kk % 2
                            sc.op('pe', lambda e: e.matmul(ps[:, sb, lo:512], kb_.ap[:, c, kk * 128:(kk + 1) * 128],
                                                           qT.ap[:, qch, lo:512], start=True, stop=True),
                                  reads=kb_.pages() + qT.cp(qch), writes=PSB(sb))
                        s_mm(0)
                        for kk in range(nkb):
                            if kk + 1 < nkb:
                                s_mm(kk + 1)
                            m = max(0, kk - 4 * j)
                            lo = m * 128
                            sb = kk % 2
                            pt = pT[kk % 4]
                            sc.op('act', lambda e, sb=sb, lo=lo, pt=pt: e.activation(
                                out=pt.ap[:, lo:512], in_=ps[:, sb, lo:512], func=AF.Exp),
                                reads=PSB(sb), writes=pt.pages())
                            if kk >= 4 * j:
                                sc.op('dve', lambda e, lo=lo, pt=pt: e.tensor_tensor(
                                    out=pt.ap[:, lo:lo + 128], in0=pt.ap[:, lo:lo + 128], in1=tri_bf, op=ALU.mult),
                                    reads=pt.pages() + CBF, writes=pt.pages())

                            def pv(e, kk=kk, lo=lo, pt=pt, ob=ob, vb_=vb_, nkb=nkb):
                                e.matmul(ps[:, ob, lo:512], vb_.ap[:, kk, 0:128], pt.ap[:, lo:512],
                                         start=(kk == 0), stop=(kk == nkb - 1))
                                e.matmul(ps[:, ob + 1, lo:512], vb_.ap[:, kk, 128:256], pt.ap[:, lo:512],
                                         start=(kk == 0), stop=(kk == nkb - 1))
                                return e.matmul(ps[:, ob + 2, lo:512], ones_bf, pt.ap[:, lo:512],
                                                start=(kk == 0), stop=(kk == nkb - 1))
                            sc.op('pe', pv, reads=pt.pages() + vb_.pages() + CBF, writes=PSB(ob, 3))
                        LT = [('lamt',)]
                        if c == 0:
                            if h > 0:
                                emit_subln(h - 1)
                            sc.op('dve', lambda e: e.reciprocal(out=r1.ap, in_=ps[:, 4, :]), reads=PSB(4),
                                  writes=r1.pages())
                            for ec in range(2):
                                sc.op('dve', lambda e, ec=ec: e.tensor_tensor(out=att.ap[:, ec, :], in0=ps[:, 2 + ec, :],
                                                                              in1=r1.ap, op=ALU.mult),
                                      reads=PSB(2 + ec) + r1.pages(), writes=att.cp(ec))
                        else:
                            sc.op('dve', lambda e: e.reciprocal(out=r2.ap, in_=ps[:, 7, :]), reads=PSB(7),
                                  writes=r2.pages())
                            sc.op('dve', lambda e, l=l: e.tensor_scalar(
                                out=r2.ap, in0=r2.ap, scalar1=lamt[:, l * 4 + 3:l * 4 + 4], scalar2=None, op0=ALU.mult),
                                reads=r2.pages() + LT, writes=r2.pages())
                            for ec in range(2):
                                sc.op('dve', lambda e, ec=ec: e.tensor_tensor(out=t1b.ap[:, ec, :], in0=ps[:, 5 + ec, :],
                                                                              in1=r2.ap, op=ALU.mult),
                                      reads=PSB(5 + ec) + r2.pages(), writes=t1b.cp(ec))
                                sc.op('dve', lambda e, ec=ec: e.tensor_tensor(out=att.ap[:, ec, :], in0=att.ap[:, ec, :],
                                                                              in1=t1b.ap[:, ec, :], op=ALU.add),
                                      reads=att.cp(ec) + t1b.cp(ec), writes=att.cp(ec))
                emit_subln(7)
                state['psg'] = 0

                for cg in range(8):
                    cga = 3 * AW + 2 * RW + cg * 512
                    cgr = cga + D

                    def ev_ga(b0, cg=cg, gi=0, b=b):
                        for n in range(4):
                            blk = cg * 4 + n
                            sc.op('act', lambda e, n=n, blk=blk: e.activation(
                                out=sgate.ap[:, n, :], in_=ps[:, b0 + n, :], func=AF.Sigmoid,
                                bias=pp[:, b + 64 + gi * 32 + blk:b + 65 + gi * 32 + blk], scale=1.0),
                                reads=PSB(b0 + n) + PP, writes=sgate.cp(n))
                    gemm(lambda kb, c0=cga, l=l: w_in_d[l, kb * 1024:(kb + 1) * 1024, c0:c0 + 512], 4,
                         hT_chunk, hT_pages, ev_ga)

                    def ev_ya(b0):
                        for n in range(4):
                            sc.op('dve', lambda e, n=n: e.tensor_tensor(out=mtmp.ap[:, n, :], in0=ps[:, b0 + n, :],
                                                                        in1=sgate.ap[:, n, :], op=ALU.mult),
                                  reads=PSB(b0 + n) + sgate.cp(n), writes=mtmp.cp(n))
                    gemm(lambda kb, cg=cg, l=l: w_ap_d[l, kb * 1024:(kb + 1) * 1024, cg * 512:(cg + 1) * 512], 2,
                         lambda k: qT.ap[:, k, :], lambda kb: qT.cp(kb * 8, 8), ev_ya)
                    gemm(lambda kb, c0=cgr, l=l: w_in_d[l, kb * 1024:(kb + 1) * 1024, c0:c0 + 512], 4,
                         hT_chunk, hT_pages, lambda b0, cg=cg: ev_ga(b0, cg, 1))

                    def ev_yr(b0, cg=cg):
                        for n in range(4):
                            blk = cg * 4 + n
                            sc.op('dve', lambda e, n=n: e.tensor_tensor(
                                out=t1b.ap[:, n % 2, :], in0=ps[:, b0 + n, :], in1=sgate.ap[:, n, :], op=ALU.mult),
                                reads=PSB(b0 + n) + sgate.cp(n), writes=t1b.cp(n % 2))
                            sc.op('dve', lambda e, n=n, blk=blk: e.tensor_tensor(
                                out=mT.ap[:, blk, :], in0=t1b.ap[:, n % 2, :], in1=mtmp.ap[:, n, :], op=ALU.add),
                                reads=t1b.cp(n % 2) + mtmp.cp(n), writes=mT.cp(blk))
                    gemm(lambda kb, cg=cg, l=l: w_rp_d[l, kb * 1024:(kb + 1) * 1024, cg * 512:(cg + 1) * 512], 1,
                         lambda k: rec.ap[:, k, :], lambda kb: rec.pages(), ev_yr)

                def ev_res(b0, cg):
                    for n in range(4):
                        blk = cg * 4 + n
                        sc.op('dve', lambda e, n=n, blk=blk: e.tensor_tensor(
                            out=xacc.ap[:, blk, :], in0=ps[:, b0 + n, :], in1=xacc.ap[:, blk, :], op=ALU.add),
                            reads=PSB(b0 + n) + xacc.cp(blk), writes=xacc.cp(blk))
                for cg in range(8):
                    gemm(lambda kb, cg=cg, l=l: w_out_d[l, kb * 1024:(kb + 1) * 1024, cg * 512:(cg + 1) * 512], 4,
                         mT_chunk, mT_pages, lambda b0, cg=cg: ev_res(b0, cg))

                rmsnorm_to_hT(b + 32)
                for ffg in range(4):
                    for cg in range(8):
                        c0 = ffg * 4096 + cg * 512

                        def ev_up(b0, cg=cg):
                            for n in range(4):
                                blk = cg * 4 + n
                                r = rl[n % 2]
                                sc.op('act', lambda e, n=n, r=r: e.activation(out=r.ap, in_=ps[:, b0 + n, :], func=AF.Relu),
                                      reads=PSB(b0 + n), writes=r.pages())
                                sc.op('dve', lambda e, r=r, blk=blk: e.tensor_tensor(out=mT.ap[:, blk, :], in0=r.ap,
                                                                                     in1=r.ap, op=ALU.mult),
                                      reads=r.pages(), writes=mT.cp(blk))
                        gemm(lambda kb, c0=c0, l=l: w_up_d[l, kb * 1024:(kb + 1) * 1024, c0:c0 + 512], 4,
                             hT_chunk, hT_pages, ev_up)
                    for cg in range(8):
                        r0 = ffg * 4096
                        gemm(lambda kb, cg=cg, l=l, r0=r0: w_dn_d[l, r0 + kb * 1024:r0 + (kb + 1) * 1024,
                                                                  cg * 512:(cg + 1) * 512], 4,
                             mT_chunk, mT_pages, lambda b0, cg=cg: ev_res(b0, cg))

            bank = 4 * state['psg']
            state['psg'] ^= 1
            for c in range(NCH):
                s = sq[c % 2]
                sc.op('act', lambda e, c=c, s=s: e.activation(out=s.ap, in_=xacc.ap[:, c, :], func=AF.Square),
                      reads=xacc.cp(c), writes=s.pages())
                sc.op('pe', lambda e, c=c, s=s, bank=bank: e.matmul(ps[:, bank, :], onesf, s.ap, start=(c == 0),
                                                                    stop=(c == NCH - 1)),
                      reads=s.pages() + CONST, writes=PSB(bank))
            sc.op('act', lambda e, bank=bank: e.activation(out=rstd.ap, in_=ps[:, bank, :], func=AF.Sqrt, bias=1.0,
                                                           scale=1.0 / (D * EPS)), reads=PSB(bank), writes=rstd.pages())
            sc.op('dve', lambda e: e.reciprocal(out=rstd.ap, in_=rstd.ap), reads=rstd.pages(), writes=rstd.pages())
            for c in range(NCH):
                sc.op('dve', lambda e, c=c: e.scalar_tensor_tensor(
                    out=xacc.ap[:, c, :], in0=xacc.ap[:, c, :], scalar=pp[:, fb + c:fb + c + 1], in1=rstd.ap,
                    op0=ALU.mult, op1=ALU.mult), reads=xacc.cp(c) + rstd.pages() + PP, writes=xacc.cp(c))
            for tb in range(4):
                xs = xstage[tb % 2]
                for c4 in range(8):
                    bank = (c4 % 2) * 4 + (c4 // 2) % 4

                    def tr2(e, c4=c4, tb=tb, bank=bank):
                        last = None
                        for q in range(4):
                            last = e.transpose(ps[:, bank, q * 128:(q + 1) * 128],
                                               xacc.ap[:, c4 * 4 + q, tb * 128:(tb + 1) * 128], identf)
                        return last
                    sc.op('pe', tr2, reads=xacc.cp(c4 * 4, 4) + CONST, writes=PSB(bank))
                    copy_op(evac_eng(), xs.ap[:, c4 * 4:(c4 + 1) * 4, :],
                            ps[:, bank, :].rearrange("p (a b) -> p a b", a=4),
                            reads=PSB(bank), writes=xs.pages(c4 * 2048, (c4 + 1) * 2048))
                r0 = t0 + tb * 128
                sc.op('sp', lambda e, xs=xs, r0=r0: e.dma_start(
                    out=out_d[r0:r0 + 128, :], in_=xs.ap.rearrange("p a b -> p (a b)")),
                    reads=xs.pages(), writes=[('out', r0)])

        sc.emit(block)
    return nc


def pack_params(inp, L):
    def pc(v):
        v = np.asarray(v, dtype=np.float32)
        return v.reshape(-1, 128).T
    cols = []
    for l in range(4):
        if l < L:
            cols += [pc(inp["norm1_g"][l]), pc(inp["norm2_g"][l]), pc(inp["gate_b"][l, 0]), pc(inp["gate_b"][l, 1])]
            cols += [pc(inp["conv_w"][l, jj]) for jj in range(4)]
            cols += [pc(inp["conv_b"][l]), pc(inp["b_rgate"][l]), pc(inp["b_igate"][l]), pc(inp["lru_lambda"][l]),
                     pc(inp["subln_g"][l])]
        else:
            cols.append(np.ones((128, PPL), np.float32))
    cols.append(pc(inp["final_g"]))
    return np.ascontiguousarray(np.concatenate(cols, axis=1))


def make_consts():
    c = np.zeros((128, 384), np.float32)
    c[:, 0:128] = np.eye(128, dtype=np.float32)
    c[:, 128:256] = np.triu(np.ones((128, 128), np.float32))
    c[:, 256:384] = 1.0
    return c


def kernel(**inputs):
    inp = {k: np.asarray(v) for k, v in inputs.items()}
    L, NT, NCORES = 4, 4, 8
    nc = build(L, NT)
    pp = pack_params(inp, L)
    consts = make_consts()
    shared = {
        "w_in": inp["w_in"], "w_attn_proj": inp["w_attn_proj"], "w_rec_proj": inp["w_rec_proj"],
        "w_out": inp["w_out"], "w_up": inp["w_up"], "w_down": inp["w_down"],
        "w_rgate": inp["w_rgate"], "w_igate": inp["w_igate"], "pp": pp,
        "lam_qk": np.ascontiguousarray(inp["lam_qk"], dtype=np.float32), "consts": consts,
    }
    in_maps = []
    for c in range(NCORES):
        m = dict(shared)
        m["x"] = np.ascontiguousarray(inp["x"][c], dtype=np.float32)
        in_maps.append(m)
    res = run_bass_kernel_spmd(nc, in_maps, core_ids=list(range(NCORES)))
    return np.stack([np.asarray(r["out"]) for r in res.results], axis=0).astype(np.float32)
```

```python
import math
from contextlib import ExitStack
import numpy as np
import concourse.bass as bass
import concourse.mybir as mybir
from concourse.bass_utils import run_bass_kernel_spmd

F32 = mybir.dt.float32
BF16 = mybir.dt.bfloat16
U8 = mybir.dt.uint8
AF = mybir.ActivationFunctionType
ALU = mybir.AluOpType
AX = mybir.AxisListType

S = 2048
D = 4096
T = 512
AW = 2048
RW = 1024
FF = 16384
CIN = 16384
EPS = 1e-6
NCH = D // 128
PPL = 194
KB = 1024


class Sched:
    def __init__(self, nc, stack):
        self.nc = nc
        self.engs = ['pe', 'act', 'dve', 'sp', 'gq']
        self.ops = {e: [] for e in self.engs}
        self.cnt = {e: 0 for e in self.engs}
        self.sem = {e: stack.enter_context(nc.semaphore('s_' + e)) for e in ['pe', 'act', 'dve']}
        self.NDS = 16
        self.dsem = {q: [stack.enter_context(nc.semaphore('d_%s%d' % (q, i))) for i in range(self.NDS)]
                     for q in ['sp', 'gq']}
        self.dval = {q: [0] * self.NDS for q in ['sp', 'gq']}
        self.dnext = {q: 0 for q in ['sp', 'gq']}
        self.waited = {e: {} for e in self.engs}
        self.lastw = {}
        self.readers = {}

    def op(self, eng, fn, reads=(), writes=()):
        deps = {}

        def add(tok):
            if tok is None:
                return
            teng, sem, val = tok
            if teng == 'pe' and eng == 'pe':
                return
            key = id(sem)
            if key not in deps or deps[key][1] < val:
                deps[key] = (sem, val)
        for p in reads:
            add(self.lastw.get(p))
        for p in writes:
            add(self.lastw.get(p))
            for t in self.readers.get(p, {}).values():
                add(t)
        if eng in ('sp', 'gq'):
            k = self.dnext[eng]
            self.dnext[eng] = (k + 1) % self.NDS
            sem = self.dsem[eng][k]
            prev = self.dval[eng][k]
            if prev > 0:
                add(('dma_prev', sem, prev))
            self.dval[eng][k] = prev + 16
            tok = (eng, sem, prev + 16)
            inc = 16
        else:
            self.cnt[eng] += 1
            sem = self.sem[eng]
            tok = (eng, sem, self.cnt[eng])
            inc = 1
        waits = []
        w = self.waited[eng]
        for key, (s, v) in deps.items():
            if w.get(key, 0) >= v:
                continue
            w[key] = v
            waits.append((s, v))
        self.ops[eng].append((waits, fn, sem, inc))
        for p in writes:
            self.lastw[p] = tok
            self.readers[p] = {}
        for p in reads:
            self.readers.setdefault(p, {})[id(tok[1])] = tok
        return tok

    def emit(self, block):
        def mk(name):
            def body(e):
                for waits, fn, sem, inc in self.ops[name]:
                    for s, v in waits:
                        e.wait_ge(s, v)
                    ins = fn(e)
                    ins.then_inc(sem, inc)
                if name == 'sp':
                    for q in ('sp', 'gq'):
                        for k in range(self.NDS):
                            if self.dval[q][k] > 0:
                                e.wait_ge(self.dsem[q][k], self.dval[q][k])
            return body
        block.tensor(mk('pe'))
        block.scalar(mk('act'))
        block.vector(mk('dve'))
        block.sync(mk('sp'))
        block.gpsimd(mk('gq'))


class Buf:
    def __init__(self, arena, off, shape, dtype):
        esz = 4 if dtype == F32 else 2
        n = 1
        for s in shape:
            n *= s
        self.off = off
        self.nbytes = n * esz
        ap = arena[:, off:off + n * esz].bitcast(dtype)
        if len(shape) == 2:
            ap = ap.rearrange("p (a b) -> p a b", a=shape[0])
        elif len(shape) == 3:
            ap = ap.rearrange("p (a b c) -> p a b c", a=shape[0], b=shape[1])
        self.ap = ap
        self.shape = shape

    def pages(self, lo=None, hi=None):
        lo = 0 if lo is None else lo
        hi = self.nbytes if hi is None else hi
        return [('a', i) for i in range((self.off + lo) // KB, (self.off + hi - 1) // KB + 1)]

    def cp(self, c, n=1):
        per = self.nbytes // self.shape[0]
        return self.pages(c * per, (c + n) * per)


def build(L, NT, n_in_layers=None):
    nc = bass.Bass("TRN2", target_bir_lowering=False)
    LW = L if n_in_layers is None else n_in_layers
    x_d = nc.dram_tensor("x", [S, D], F32, kind="ExternalInput").ap()
    w_in_d = nc.dram_tensor("w_in", [LW, D, CIN], F32, kind="ExternalInput").ap()
    w_ap_d = nc.dram_tensor("w_attn_proj", [LW, AW, D], F32, kind="ExternalInput").ap()
    w_rp_d = nc.dram_tensor("w_rec_proj", [LW, RW, D], F32, kind="ExternalInput").ap()
    w_out_d = nc.dram_tensor("w_out", [LW, D, D], F32, kind="ExternalInput").ap()
    w_up_d = nc.dram_tensor("w_up", [LW, D, FF], F32, kind="ExternalInput").ap()
    w_dn_d = nc.dram_tensor("w_down", [LW, FF, D], F32, kind="ExternalInput").ap()
    w_rg_d = nc.dram_tensor("w_rgate", [LW, 8, 128, 128], F32, kind="ExternalInput").ap()
    w_ig_d = nc.dram_tensor("w_igate", [LW, 8, 128, 128], F32, kind="ExternalInput").ap()
    pp_d = nc.dram_tensor("pp", [128, PPL * 4 + 32], F32, kind="ExternalInput").ap()
    lamqk_d = nc.dram_tensor("lam_qk", [4, 4, 128], F32, kind="ExternalInput").ap()
    consts_d = nc.dram_tensor("consts", [128, 384], F32, kind="ExternalInput").ap()
    out_d = nc.dram_tensor("out", [S, D], F32, kind="ExternalOutput").ap()
    kT_d = nc.dram_tensor("kT_cache", [L, AW, S], BF16).ap()
    vC_d = nc.dram_tensor("v_cache", [L, S, AW], BF16).ap()

    with ExitStack() as st:
        ARENA = 198 * KB
        arena = st.enter_context(nc.sbuf_tensor("arena", [128, ARENA], U8))
        ident = st.enter_context(nc.sbuf_tensor("ident", [128, 384], F32))
        cbf = st.enter_context(nc.sbuf_tensor("cbf", [128, 256], BF16))
        pp = st.enter_context(nc.sbuf_tensor("ppt", [128, PPL * 4 + 32], F32))
        lamt = st.enter_context(nc.sbuf_tensor("lamt", [128, 16], F32))
        hst = st.enter_context(nc.sbuf_tensor("hst", [128, 4 * 8], F32))
        chist = st.enter_context(nc.sbuf_tensor("chist", [128, 4 * 8 * 3], F32))
        ps = st.enter_context(nc.psum_tensor("ps", [128, 8, 512], F32))
        sc = Sched(nc, st)
        block = st.enter_context(nc.Block())

        identf = ident[:, 0:128]
        onesf = ident[:, 256:384]
        tri_bf = cbf[:, 0:128]
        ones_bf = cbf[:, 128:256]

        xacc = Buf(arena, 0, (32, 512), F32)
        hT = Buf(arena, 64 * KB, (32, 512), BF16)
        mT = Buf(arena, 96 * KB, (32, 512), BF16)
        kbuf = [Buf(arena, 96 * KB + s * 8 * KB, (2, 2048), BF16) for s in range(2)]
        vbuf = [Buf(arena, 112 * KB + s * 8 * KB, (16, 256), BF16) for s in range(2)]
        xstage = [Buf(arena, 96 * KB + s * 16 * KB, (32, 128), F32) for s in range(2)]
        vst = Buf(arena, 96 * KB, (4, 2048), BF16)
        wg = Buf(arena, 96 * KB, (2, 8, 128), BF16)
        gel = Buf(arena, 104 * KB, (8, 512), BF16)
        ltmp = [Buf(arena, 112 * KB + i * 2 * KB, (512,), F32) for i in range(8)]
        qT = Buf(arena, 128 * KB, (16, 512), BF16)
        NWB = 3
        wb = [Buf(arena, 144 * KB + s * 8 * KB, (8, 512), BF16) for s in range(NWB)]
        rec = Buf(arena, 168 * KB, (8, 512), BF16)
        zlx = Buf(arena, 176 * KB, (8, 515), F32)
        kst = Buf(arena, 176 * KB, (16, 512), BF16)
        pT = [Buf(arena, 176 * KB + i * KB, (512,), BF16) for i in range(4)]
        r1 = Buf(arena, 180 * KB, (512,), F32)
        r2 = Buf(arena, 182 * KB, (512,), F32)
        att = Buf(arena, 184 * KB, (2, 512), F32)
        t1b = Buf(arena, 188 * KB, (2, 512), F32)
        sgate = Buf(arena, 176 * KB, (4, 512), BF16)
        mtmp = Buf(arena, 180 * KB, (4, 512), F32)
        rl = [Buf(arena, 176 * KB + i * 2 * KB, (512,), F32) for i in range(2)]
        sq = [Buf(arena, 192 * KB + i * 2 * KB, (512,), F32) for i in range(2)]
        rstd = Buf(arena, 196 * KB, (512,), F32)
        lqb = Buf(arena, 0, (2048,), F32)

        PSB = lambda b, n=1: [('ps', i) for i in range(b, b + n)]
        PP = [('pp',)]
        CONST = [('const',)]

        sc.op('sp', lambda e: e.dma_start(out=ident[:], in_=consts_d[:, :]), writes=CONST)
        sc.op('sp', lambda e: e.dma_start(out=pp[:], in_=pp_d[:, :]), writes=PP)
        sc.op('sp', lambda e: e.dma_start(
            out=lqb.ap, in_=lamqk_d.rearrange("l f d -> (l f d)").partition_broadcast(128)),
            writes=lqb.pages())
        sc.op('dve', lambda e: e.tensor_copy(out=cbf[:, 0:128], in_=ident[:, 128:256]), reads=CONST, writes=[('cbf',)])
        sc.op('dve', lambda e: e.tensor_copy(out=cbf[:, 128:256], in_=ident[:, 256:384]), reads=CONST, writes=[('cbf',)])
        CBF = [('cbf',)]
        sc.op('dve', lambda e: e.memset(hst[:], 0.0), writes=[('hst',)])
        sc.op('dve', lambda e: e.memset(chist[:], 0.0), writes=[('chist',)])
        for l in range(L):
            b = l * PPL
            lam_init = 0.8 - 0.6 * math.exp(-0.3 * l)
            sc.op('dve', lambda e, b=b: e.tensor_scalar(out=pp[:, b:b + 64], in0=pp[:, b:b + 64], scalar1=EPS ** -0.5,
                                                        scalar2=None, op0=ALU.mult), reads=PP, writes=PP)
            sc.op('act', lambda e, b=b: e.activation(out=pp[:, b + 184:b + 192], in_=pp[:, b + 184:b + 192],
                                                     func=AF.Exp, scale=-1.0), reads=PP, writes=PP)
            sc.op('act', lambda e, b=b: e.activation(out=pp[:, b + 184:b + 192], in_=pp[:, b + 184:b + 192],
                                                     func=AF.Ln, bias=1.0, scale=1.0), reads=PP, writes=PP)
            sc.op('dve', lambda e, b=b: e.tensor_scalar(out=pp[:, b + 184:b + 192], in0=pp[:, b + 184:b + 192],
                                                        scalar1=-8.0, scalar2=None, op0=ALU.mult), reads=PP, writes=PP)
            sc.op('dve', lambda e, b=b, c=(EPS ** -0.5) * (1.0 - lam_init): e.tensor_scalar(
                out=pp[:, b + 192:b + 194], in0=pp[:, b + 192:b + 194], scalar1=c, scalar2=None, op0=ALU.mult),
                reads=PP, writes=PP)
            LT = [('lamt',)]
            for hf in range(2):
                o0 = l * 512 + hf * 256
                sc.op('dve', lambda e, o0=o0: e.tensor_tensor(out=lqb.ap[:, o0:o0 + 128], in0=lqb.ap[:, o0:o0 + 128],
                                                              in1=lqb.ap[:, o0 + 128:o0 + 256], op=ALU.mult),
                      reads=lqb.pages(), writes=lqb.pages())
                sc.op('dve', lambda e, o0=o0, l=l, hf=hf: e.reduce_sum(out=lamt[:, l * 4 + hf:l * 4 + hf + 1],
                                                                      in_=lqb.ap[:, o0:o0 + 128], axis=AX.X),
                      reads=lqb.pages(), writes=LT)
            sc.op('act', lambda e, l=l: e.activation(out=lamt[:, l * 4:l * 4 + 2], in_=lamt[:, l * 4:l * 4 + 2],
                                                     func=AF.Exp), reads=LT, writes=LT)
            sc.op('dve', lambda e, l=l: e.tensor_tensor(out=lamt[:, l * 4 + 2:l * 4 + 3], in0=lamt[:, l * 4:l * 4 + 1],
                                                        in1=lamt[:, l * 4 + 1:l * 4 + 2], op=ALU.subtract),
                  reads=LT, writes=LT)
            sc.op('dve', lambda e, l=l, li=lam_init: e.tensor_scalar(
                out=lamt[:, l * 4 + 3:l * 4 + 4], in0=lamt[:, l * 4 + 2:l * 4 + 3], scalar1=li, scalar2=-1.0,
                op0=ALU.add, op1=ALU.mult), reads=LT, writes=LT)
        fb = 4 * PPL
        sc.op('dve', lambda e: e.tensor_scalar(out=pp[:, fb:fb + 32], in0=pp[:, fb:fb + 32], scalar1=EPS ** -0.5,
                                               scalar2=None, op0=ALU.mult), reads=PP, writes=PP)

        state = {'wslot': 0, 'psg': 0, 'cp': 0}

        def evac_eng():
            state['cp'] ^= 1
            return 'act' if state['cp'] else 'dve'

        def copy_op(eng, out, in_, reads, writes, scale=None):
            if eng == 'act':
                if scale is None:
                    sc.op('act', lambda e: e.activation(out=out, in_=in_, func=AF.Copy), reads=reads, writes=writes)
                else:
                    sc.op('act', lambda e: e.activation(out=out, in_=in_, func=AF.Copy, scale=scale),
                          reads=reads, writes=writes)
            else:
                if scale is None:
                    sc.op('dve', lambda e: e.tensor_copy(out=out, in_=in_), reads=reads, writes=writes)
                else:
                    sc.op('dve', lambda e: e.tensor_scalar(out=out, in0=in_, scalar1=scale, scalar2=None,
                                                           op0=ALU.mult), reads=reads, writes=writes)

        def gemm(wsrc, nk, act_chunk, act_pages, evac, mode='B', defer=False):
            g = state['psg']
            state['psg'] ^= 1
            b0 = 4 * g
            for kb in range(nk):
                s = state['wslot']
                state['wslot'] = (s + 1) % NWB
                src = wsrc(kb).rearrange("(kc p) c -> p kc c", p=128)
                sc.op('gq', lambda e, s=s, src=src: e.dma_start(out=wb[s].ap, in_=src), writes=wb[s].pages())

                def mm(e, s=s, kb=kb):
                    last = None
                    for kc in range(8):
                        k = kb * 8 + kc
                        for n in range(4):
                            if mode == 'B':
                                last = e.matmul(ps[:, b0 + n, :], wb[s].ap[:, kc, n * 128:(n + 1) * 128],
                                                act_chunk(k), start=(k == 0), stop=(k == nk * 8 - 1))
                            else:
                                last = e.matmul(ps[:, b0 + n, :], act_chunk(k)[:, n * 128:(n + 1) * 128],
                                                wb[s].ap[:, kc, :], start=(k == 0), stop=(k == nk * 8 - 1))
                    return last
                sc.op('pe', mm, reads=wb[s].pages() + act_pages(kb), writes=PSB(b0, 4))
            if defer:
                return lambda: evac(b0)
            evac(b0)

        def rmsnorm_to_hT(gcol):
            bank = 4 * state['psg']
            state['psg'] ^= 1
            for c in range(NCH):
                s = sq[c % 2]
                sb16 = s.ap[:, 0:256].bitcast(BF16)
                if c % 2 == 0:
                    sc.op('act', lambda e, c=c, sb16=sb16: e.activation(out=sb16, in_=xacc.ap[:, c, :], func=AF.Square),
                          reads=xacc.cp(c), writes=s.pages())
                else:
                    sc.op('dve', lambda e, c=c, sb16=sb16: e.tensor_tensor(out=sb16, in0=xacc.ap[:, c, :],
                                                                           in1=xacc.ap[:, c, :], op=ALU.mult),
                          reads=xacc.cp(c), writes=s.pages())
                sc.op('pe', lambda e, c=c, sb16=sb16: e.matmul(ps[:, bank, :], ones_bf, sb16, start=(c == 0),
                                                               stop=(c == NCH - 1)),
                      reads=s.pages() + CBF, writes=PSB(bank))
            sc.op('act', lambda e: e.activation(out=rstd.ap, in_=ps[:, bank, :], func=AF.Sqrt, bias=1.0,
                                                scale=1.0 / (D * EPS)), reads=PSB(bank), writes=rstd.pages())
            sc.op('dve', lambda e: e.reciprocal(out=rstd.ap, in_=rstd.ap), reads=rstd.pages(), writes=rstd.pages())
            for c in range(NCH):
                sc.op('dve', lambda e, c=c: e.scalar_tensor_tensor(
                    out=hT.ap[:, c, :], in0=xacc.ap[:, c, :], scalar=pp[:, gcol + c:gcol + c + 1], in1=rstd.ap,
                    op0=ALU.mult, op1=ALU.mult), reads=xacc.cp(c) + rstd.pages() + PP, writes=hT.cp(c))

        hT_chunk = lambda k: hT.ap[:, k, :]
        hT_pages = lambda kb: hT.cp(kb * 8, 8)
        mT_chunk = lambda k: mT.ap[:, k, :]
        mT_pages = lambda kb: mT.cp(kb * 8, 8)

        for j in range(NT):
            t0 = j * T
            for tb in range(4):
                xs = xstage[tb % 2]
                r0 = t0 + tb * 128
                sc.op('sp', lambda e, xs=xs, r0=r0: e.dma_start(
                    out=xs.ap.rearrange("p a b -> p (a b)"), in_=x_d[r0:r0 + 128, :]), writes=xs.pages())
                for c4 in range(8):
                    bank = (c4 % 2) * 4 + (c4 // 2) % 4

                    def tr(e, xs=xs, c4=c4, bank=bank):
                        last = None
                        for q in range(4):
                            last = e.transpose(ps[:, bank, q * 128:(q + 1) * 128], xs.ap[:, c4 * 4 + q, :], identf)
                        return last
                    sc.op('pe', tr, reads=xs.pages() + CONST, writes=PSB(bank))
                    eng = evac_eng()
                    copy_op(eng, xacc.ap[:, c4 * 4:(c4 + 1) * 4, tb * 128:(tb + 1) * 128],
                            ps[:, bank, :].rearrange("p (a b) -> p a b", a=4),
                            reads=PSB(bank), writes=xacc.cp(c4 * 4, 4))

            for l in range(L):
                b = l * PPL
                rmsnorm_to_hT(b)

                sc.op('gq', lambda e, l=l: e.dma_start(out=wg.ap[:, 0], in_=w_rg_d[l].rearrange("n c d -> c n d")),
                      writes=wg.pages())
                sc.op('gq', lambda e, l=l: e.dma_start(out=wg.ap[:, 1], in_=w_ig_d[l].rearrange("n c d -> c n d")),
                      writes=wg.pages())
                CH = [('chist',)]
                HS = [('hst',)]
                sc.op('dve', lambda e, l=l: e.tensor_copy(
                    out=zlx.ap[:, :, 0:3], in_=chist[:, l * 24:(l + 1) * 24].rearrange("p (a b) -> p a b", a=8)),
                    reads=CH, writes=zlx.pages())
                for cg in range(2):
                    c0 = 3 * AW + cg * 512

                    def ev_lx(b0, cg=cg):
                        for n in range(4):
                            ch = cg * 4 + n
                            copy_op(evac_eng(), zlx.ap[:, ch, 3:515], ps[:, b0 + n, :], reads=PSB(b0 + n),
                                    writes=zlx.pages())
                    gemm(lambda kb, c0=c0, l=l: w_in_d[l, kb * 1024:(kb + 1) * 1024, c0:c0 + 512], 4,
                         hT_chunk, hT_pages, ev_lx)
                for cg in range(2):
                    c0 = 3 * AW + RW + cg * 512

                    def ev_ly(b0, cg=cg):
                        for n in range(4):
                            ch = cg * 4 + n
                            ta, tb_ = ltmp[(n % 2) * 2], ltmp[(n % 2) * 2 + 1]
                            P = PSB(b0 + n)
                            sc.op('act', lambda e, ta=ta, b0=b0, n=n: e.activation(
                                out=ta.ap, in_=ps[:, b0 + n, :], func=AF.Square), reads=P, writes=ta.pages())
                            sc.op('dve', lambda e, ta=ta: e.tensor_scalar(
                                out=ta.ap, in0=ta.ap, scalar1=0.044715, scalar2=1.0, op0=ALU.mult, op1=ALU.add),
                                reads=ta.pages(), writes=ta.pages())
                            sc.op('dve', lambda e, ta=ta, b0=b0, n=n: e.tensor_tensor(
                                out=ta.ap, in0=ta.ap, in1=ps[:, b0 + n, :], op=ALU.mult),
                                reads=ta.pages() + P, writes=ta.pages())
                            sc.op('act', lambda e, ta=ta, tb_=tb_: e.activation(
                                out=tb_.ap, in_=ta.ap, func=AF.Sigmoid, scale=1.5957691216057308),
                                reads=ta.pages(), writes=tb_.pages())
                            sc.op('dve', lambda e, tb_=tb_, b0=b0, n=n, ch=ch: e.tensor_tensor(
                                out=gel.ap[:, ch, :], in0=tb_.ap, in1=ps[:, b0 + n, :], op=ALU.mult),
                                reads=tb_.pages() + P, writes=gel.cp(ch))
                    gemm(lambda kb, c0=c0, l=l: w_in_d[l, kb * 1024:(kb + 1) * 1024, c0:c0 + 512], 4,
                         hT_chunk, hT_pages, ev_ly)
                def lru_chunk(n, bank, part, l=l, b=b):
                    xr0, xrbs, tr_, ti, ta, tm, th, xr1 = ltmp
                    xr = xr0 if n % 2 == 0 else xr1
                    half = (n % 2) * 256
                    xrb16 = xrbs.ap[:, half:half + 256].bitcast(BF16)
                    xrb_pages = xrbs.pages((n % 2) * 1024, (n % 2 + 1) * 1024)
                    if part == 1:
                        return lru_part2(n, bank, xr, l, b)
                    cw = lambda jj, n=n, b=b: pp[:, b + 128 + jj * 8 + n:b + 128 + jj * 8 + n + 1]
                    col = lambda o, n=n, b=b: pp[:, b + o + n:b + o + n + 1]
                    ZP = zlx.pages()
                    sc.op('dve', lambda e, n=n, cw=cw, col=col, xr=xr: e.tensor_scalar(
                        out=xr.ap, in0=zlx.ap[:, n, 0:512], scalar1=cw(0), scalar2=col(160), op0=ALU.mult, op1=ALU.add),
                        reads=ZP + PP, writes=xr.pages())
                    for jj in range(1, 4):
                        sc.op('dve', lambda e, n=n, jj=jj, cw=cw, xr=xr: e.scalar_tensor_tensor(
                            out=xr.ap, in0=zlx.ap[:, n, jj:jj + 512], scalar=cw(jj), in1=xr.ap, op0=ALU.mult, op1=ALU.add),
                            reads=ZP + PP + xr.pages(), writes=xr.pages())
                    sc.op('act', lambda e, xr=xr, xrb16=xrb16: e.activation(out=xrb16, in_=xr.ap, func=AF.Copy),
                          reads=xr.pages(), writes=xrb_pages)
                    sc.op('pe', lambda e, n=n, bank=bank, xrb16=xrb16: e.matmul(
                        ps[:, bank, :], wg.ap[:, 0, n, :], xrb16, start=True, stop=True),
                        reads=wg.pages() + xrb_pages, writes=PSB(bank))
                    sc.op('pe', lambda e, n=n, bank=bank, xrb16=xrb16: e.matmul(
                        ps[:, bank + 1, :], wg.ap[:, 1, n, :], xrb16, start=True, stop=True),
                        reads=wg.pages() + xrb_pages, writes=PSB(bank + 1))

                def lru_part2(n, bank, xr, l, b):
                    _, _, tr_, ti, ta, tm, th, _ = ltmp
                    col = lambda o, n=n, b=b: pp[:, b + o + n:b + o + n + 1]
                    sc.op('act', lambda e, bank=bank, col=col: e.activation(
                        out=tr_.ap, in_=ps[:, bank, :], func=AF.Sigmoid, bias=col(168), scale=1.0),
                        reads=PSB(bank) + PP, writes=tr_.pages())
                    sc.op('act', lambda e, bank=bank, col=col: e.activation(
                        out=ti.ap, in_=ps[:, bank + 1, :], func=AF.Sigmoid, bias=col(176), scale=1.0),
                        reads=PSB(bank + 1) + PP, writes=ti.pages())
                    sc.op('act', lambda e, col=col: e.activation(out=ta.ap, in_=tr_.ap, func=AF.Exp, scale=col(184)),
                          reads=tr_.pages() + PP, writes=ta.pages())
                    sc.op('dve', lambda e: e.tensor_tensor(out=tm.ap, in0=ta.ap, in1=ta.ap, op=ALU.mult),
                          reads=ta.pages(), writes=tm.pages())
                    sc.op('act', lambda e: e.activation(out=tm.ap, in_=tm.ap, func=AF.Sqrt, bias=1.0, scale=-1.0),
                          reads=tm.pages(), writes=tm.pages())
                    sc.op('dve', lambda e, xr=xr: e.tensor_tensor(out=ti.ap, in0=ti.ap, in1=xr.ap, op=ALU.mult),
                          reads=ti.pages() + xr.pages(), writes=ti.pages())
                    sc.op('dve', lambda e: e.tensor_tensor(out=tm.ap, in0=tm.ap, in1=ti.ap, op=ALU.mult),
                          reads=ti.pages() + tm.pages(), writes=tm.pages())
                    sc.op('dve', lambda e, l=l, n=n: e.tensor_tensor_scan(
                        out=th.ap, data0=ta.ap, data1=tm.ap, initial=hst[:, l * 8 + n:l * 8 + n + 1],
                        op0=ALU.mult, op1=ALU.add), reads=ta.pages() + tm.pages() + HS, writes=th.pages())
                    sc.op('dve', lambda e, l=l, n=n: e.tensor_copy(out=hst[:, l * 8 + n:l * 8 + n + 1],
                                                                   in_=th.ap[:, 511:512]),
                          reads=th.pages(), writes=HS)
                    sc.op('dve', lambda e, n=n: e.tensor_tensor(out=rec.ap[:, n, :], in0=th.ap, in1=gel.ap[:, n, :],
                                                                op=ALU.mult),
                          reads=th.pages() + gel.cp(n), writes=rec.cp(n))
                for cg in range(4):
                    c0 = cg * 512

                    def ev_q(b0, cg=cg):
                        for n in range(4):
                            ch = cg * 4 + n
                            copy_op(evac_eng(), qT.ap[:, ch, :], ps[:, b0 + n, :], reads=PSB(b0 + n),
                                    writes=qT.cp(ch), scale=128.0 ** -0.5)
                    fin = gemm(lambda kb, c0=c0, l=l: w_in_d[l, kb * 1024:(kb + 1) * 1024, c0:c0 + 512], 4,
                               hT_chunk, hT_pages, ev_q, defer=True)
                    gb = 4 * state['psg']
                    lru_chunk(2 * cg, gb, 0)
                    lru_chunk(2 * cg + 1, gb + 2, 0)
                    lru_chunk(2 * cg, gb, 1)
                    lru_chunk(2 * cg + 1, gb + 2, 1)
                    fin()
                sc.op('dve', lambda e, l=l: e.tensor_copy(
                    out=chist[:, l * 24:(l + 1) * 24].rearrange("p (a b) -> p a b", a=8), in_=zlx.ap[:, :, 512:515]),
                    reads=zlx.pages(), writes=CH)

                for cg in range(4):
                    c0 = AW + cg * 512

                    def ev_k(b0, cg=cg):
                        for n in range(4):
                            ch = cg * 4 + n
                            copy_op(evac_eng(), kst.ap[:, ch, :], ps[:, b0 + n, :], reads=PSB(b0 + n),
                                    writes=kst.cp(ch))
                    gemm(lambda kb, c0=c0, l=l: w_in_d[l, kb * 1024:(kb + 1) * 1024, c0:c0 + 512], 4,
                         hT_chunk, hT_pages, ev_k)
                KD = [('kT', l)]
                VD = [('vC', l)]
                sc.op('sp', lambda e, l=l, t0=t0: e.dma_start(
                    out=kT_d[l, :, t0:t0 + T].rearrange("(c p) t -> p c t", p=128), in_=kst.ap),
                    reads=kst.pages(), writes=KD)
                for cg in range(4):
                    c0 = 2 * AW + cg * 512

                    def ev_v(b0, cg=cg):
                        for tb in range(4):
                            copy_op(evac_eng(), vst.ap[:, tb, cg * 512:(cg + 1) * 512], ps[:, b0 + tb, :],
                                    reads=PSB(b0 + tb), writes=vst.pages(tb * 4096 + cg * 1024, tb * 4096 + (cg + 1) * 1024))
                    gemm(lambda kb, c0=c0, l=l: w_in_d[l, kb * 1024:(kb + 1) * 1024, c0:c0 + 512], 4,
                         hT_chunk, hT_pages, ev_v, mode='A')
                sc.op('sp', lambda e, l=l, t0=t0: e.dma_start(
                    out=vC_d[l, t0:t0 + T, :].rearrange("(tb p) f -> p tb f", p=128), in_=vst.ap),
                    reads=vst.pages(), writes=VD)

                nkb = 4 * (j + 1)
                nk_t = nkb * 128
                def emit_subln(h, b=b):
                    for ec in range(2):
                        s_ = sq[ec]
                        sb16 = s_.ap[:, 0:256].bitcast(BF16)
                        sc.op('act', lambda e, ec=ec, sb16=sb16: e.activation(out=sb16, in_=att.ap[:, ec, :], func=AF.Square),
                              reads=att.cp(ec), writes=s_.pages())
                        sc.op('pe', lambda e, ec=ec, sb16=sb16: e.matmul(ps[:, 5, :], ones_bf, sb16, start=(ec == 0),
                                                                         stop=(ec == 1)),
                              reads=s_.pages() + CBF, writes=PSB(5))
                    sc.op('act', lambda e: e.activation(out=rstd.ap, in_=ps[:, 5, :], func=AF.Sqrt, bias=1.0,
                                                        scale=1.0 / (256 * EPS)), reads=PSB(5), writes=rstd.pages())
                    sc.op('dve', lambda e: e.reciprocal(out=rstd.ap, in_=rstd.ap), reads=rstd.pages(),
                          writes=rstd.pages())
                    for ec in range(2):
                        sc.op('dve', lambda e, ec=ec, h=h, b=b: e.scalar_tensor_tensor(
                            out=qT.ap[:, 2 * h + ec, :], in0=att.ap[:, ec, :], scalar=pp[:, b + 192 + ec:b + 193 + ec],
                            in1=rstd.ap, op0=ALU.mult, op1=ALU.mult),
                            reads=att.cp(ec) + rstd.pages() + PP, writes=qT.cp(2 * h + ec))
                for h in range(8):
                    kb_ = kbuf[h % 2]
                    vb_ = vbuf[h % 2]
                    sc.op('sp', lambda e, l=l, h=h, kb_=kb_, nk_t=nk_t: e.dma_start(
                        out=kb_.ap[:, :, 0:nk_t],
                        in_=kT_d[l, h * 256:(h + 1) * 256, 0:nk_t].rearrange("(c p) t -> p c t", p=128)),
                        reads=KD, writes=kb_.pages())
                    sc.op('sp', lambda e, l=l, h=h, vb_=vb_, nk_t=nk_t, nkb=nkb: e.dma_start(
                        out=vb_.ap[:, 0:nkb, :],
                        in_=vC_d[l, 0:nk_t, h * 256:(h + 1) * 256].rearrange("(kb p) f -> p kb f", p=128)),
                        reads=VD, writes=vb_.pages())
                    for c in range(2):
                        ob = 2 + 3 * c
                        qch = 2 * h + c

                        def s_mm(kk, qch=qch, c=c, kb_=kb_):
                            m = max(0, kk - 4 * j)
                            lo = m * 128
                            sb = kk % 2
                            sc.op('pe', lambda e: e.matmul(ps[:, sb, lo:512], kb_.ap[:, c, kk * 128:(kk + 1) * 128],
                                                           qT.ap[:, qch, lo:512], start=True, stop=True),
                                  reads=kb_.pages() + qT.cp(qch), writes=PSB(sb))
                        s_mm(0)
                        for kk in range(nkb):
                            if kk + 1 < nkb:
                                s_mm(kk + 1)
                            m = max(0, kk - 4 * j)
                            lo = m * 128
                            sb = kk % 2
                            pt = pT[kk % 4]
                            sc.op('act', lambda e, sb=sb, lo=lo, pt=pt: e.activation(
                                out=pt.ap[:, lo:512], in_=ps[:, sb, lo:512], func=AF.Exp),
                                reads=PSB(sb), writes=pt.pages())
                            if kk >= 4 * j:
                                sc.op('dve', lambda e, lo=lo, pt=pt: e.tensor_tensor(
                                    out=pt.ap[:, lo:lo + 128], in0=pt.ap[:, lo:lo + 128], in1=tri_bf, op=ALU.mult),
                                    reads=pt.pages() + CBF, writes=pt.pages())

                            def pv(e, kk=kk, lo=lo, pt=pt, ob=ob, vb_=vb_, nkb=nkb):
                                e.matmul(ps[:, ob, lo:512], vb_.ap[:, kk, 0:128], pt.ap[:, lo:512],
                                         start=(kk == 0), stop=(kk == nkb - 1))
                                e.matmul(ps[:, ob + 1, lo:512], vb_.ap[:, kk, 128:256], pt.ap[:, lo:512],
                                         start=(kk == 0), stop=(kk == nkb - 1))
                                return e.matmul(ps[:, ob + 2, lo:512], ones_bf, pt.ap[:, lo:512],
                                                start=(kk == 0), stop=(kk == nkb - 1))
                            sc.op('pe', pv, reads=pt.pages() + vb_.pages() + CBF, writes=PSB(ob, 3))
                        LT = [('lamt',)]
                        if c == 0:
                            if h > 0:
                                emit_subln(h - 1)
                            sc.op('dve', lambda e: e.reciprocal(out=r1.ap, in_=ps[:, 4, :]), reads=PSB(4),
                                  writes=r1.pages())
                            for ec in range(2):
                                sc.op('dve', lambda e, ec=ec: e.tensor_tensor(out=att.ap[:, ec, :], in0=ps[:, 2 + ec, :],
                                                                              in1=r1.ap, op=ALU.mult),
                                      reads=PSB(2 + ec) + r1.pages(), writes=att.cp(ec))
                        else:
                            sc.op('dve', lambda e: e.reciprocal(out=r2.ap, in_=ps[:, 7, :]), reads=PSB(7),
                                  writes=r2.pages())
                            sc.op('dve', lambda e, l=l: e.tensor_scalar(
                                out=r2.ap, in0=r2.ap, scalar1=lamt[:, l * 4 + 3:l * 4 + 4], scalar2=None, op0=ALU.mult),
                                reads=r2.pages() + LT, writes=r2.pages())
                            for ec in range(2):
                                sc.op('dve', lambda e, ec=ec: e.tensor_tensor(out=t1b.ap[:, ec, :], in0=ps[:, 5 + ec, :],
                                                                              in1=r2.ap, op=ALU.mult),
                                      reads=PSB(5 + ec) + r2.pages(), writes=t1b.cp(ec))
                                sc.op('dve', lambda e, ec=ec: e.tensor_tensor(out=att.ap[:, ec, :], in0=att.ap[:, ec, :],
                                                                              in1=t1b.ap[:, ec, :], op=ALU.add),
                                      reads=att.cp(ec) + t1b.cp(ec), writes=att.cp(ec))
                emit_subln(7)
                state['psg'] = 0

                for cg in range(8):
                    cga = 3 * AW + 2 * RW + cg * 512
                    cgr = cga + D

                    def ev_ga(b0, cg=cg, gi=0, b=b):
                        for n in range(4):
                            blk = cg * 4 + n
                            sc.op('act', lambda e, n=n, blk=blk: e.activation(
                                out=sgate.ap[:, n, :], in_=ps[:, b0 + n, :], func=AF.Sigmoid,
                                bias=pp[:, b + 64 + gi * 32 + blk:b + 65 + gi * 32 + blk], scale=1.0),
                                reads=PSB(b0 + n) + PP, writes=sgate.cp(n))
                    gemm(lambda kb, c0=cga, l=l: w_in_d[l, kb * 1024:(kb + 1) * 1024, c0:c0 + 512], 4,
                         hT_chunk, hT_pages, ev_ga)

                    def ev_ya(b0):
                        for n in range(4):
                            sc.op('dve', lambda e, n=n: e.tensor_tensor(out=mtmp.ap[:, n, :], in0=ps[:, b0 + n, :],
                                                                        in1=sgate.ap[:, n, :], op=ALU.mult),
                                  reads=PSB(b0 + n) + sgate.cp(n), writes=mtmp.cp(n))
                    gemm(lambda kb, cg=cg, l=l: w_ap_d[l, kb * 1024:(kb + 1) * 1024, cg * 512:(cg + 1) * 512], 2,
                         lambda k: qT.ap[:, k, :], lambda kb: qT.cp(kb * 8, 8), ev_ya)
                    gemm(lambda kb, c0=cgr, l=l: w_in_d[l, kb * 1024:(kb + 1) * 1024, c0:c0 + 512], 4,
                         hT_chunk, hT_pages, lambda b0, cg=cg: ev_ga(b0, cg, 1))

                    def ev_yr(b0, cg=cg):
                        for n in range(4):
                            blk = cg * 4 + n
                            sc.op('dve', lambda e, n=n: e.tensor_tensor(
                                out=t1b.ap[:, n % 2, :], in0=ps[:, b0 + n, :], in1=sgate.ap[:, n, :], op=ALU.mult),
                                reads=PSB(b0 + n) + sgate.cp(n), writes=t1b.cp(n % 2))
                            sc.op('dve', lambda e, n=n, blk=blk: e.tensor_tensor(
                                out=mT.ap[:, blk, :], in0=t1b.ap[:, n % 2, :], in1=mtmp.ap[:, n, :], op=ALU.add),
                                reads=t1b.cp(n % 2) + mtmp.cp(n), writes=mT.cp(blk))
                    gemm(lambda kb, cg=cg, l=l: w_rp_d[l, kb * 1024:(kb + 1) * 1024, cg * 512:(cg + 1) * 512], 1,
                         lambda k: rec.ap[:, k, :], lambda kb: rec.pages(), ev_yr)

                def ev_res(b0, cg):
                    for n in range(4):
                        blk = cg * 4 + n
                        sc.op('dve', lambda e, n=n, blk=blk: e.tensor_tensor(
                            out=xacc.ap[:, blk, :], in0=ps[:, b0 + n, :], in1=xacc.ap[:, blk, :], op=ALU.add),
                            reads=PSB(b0 + n) + xacc.cp(blk), writes=xacc.cp(blk))
                for cg in range(8):
                    gemm(lambda kb, cg=cg, l=l: w_out_d[l, kb * 1024:(kb + 1) * 1024, cg * 512:(cg + 1) * 512], 4,
                         mT_chunk, mT_pages, lambda b0, cg=cg: ev_res(b0, cg))

                rmsnorm_to_hT(b + 32)
                for ffg in range(4):
                    for cg in range(8):
                        c0 = ffg * 4096 + cg * 512

                        def ev_up(b0, cg=cg):
                            for n in range(4):
                                blk = cg * 4 + n
                                r = rl[n % 2]
                                sc.op('act', lambda e, n=n, r=r: e.activation(out=r.ap, in_=ps[:, b0 + n, :], func=AF.Relu),
                                      reads=PSB(b0 + n), writes=r.pages())
                                sc.op('dve', lambda e, r=r, blk=blk: e.tensor_tensor(out=mT.ap[:, blk, :], in0=r.ap,
                                                                                     in1=r.ap, op=ALU.mult),
                                      reads=r.pages(), writes=mT.cp(blk))
                        gemm(lambda kb, c0=c0, l=l: w_up_d[l, kb * 1024:(kb + 1) * 1024, c0:c0 + 512], 4,
                             hT_chunk, hT_pages, ev_up)
                    for cg in range(8):
                        r0 = ffg * 4096
                        gemm(lambda kb, cg=cg, l=l, r0=r0: w_dn_d[l, r0 + kb * 1024:r0 + (kb + 1) * 1024,
                                                                  cg * 512:(cg + 1) * 512], 4,
                             mT_chunk, mT_pages, lambda b0, cg=cg: ev_res(b0, cg))

            bank = 4 * state['psg']
            state['psg'] ^= 1
            for c in range(NCH):
                s = sq[c % 2]
                sb16 = s.ap[:, 0:256].bitcast(BF16)
                sc.op('act', lambda e, c=c, sb16=sb16: e.activation(out=sb16, in_=xacc.ap[:, c, :], func=AF.Square),
                      reads=xacc.cp(c), writes=s.pages())
                sc.op('pe', lambda e, c=c, sb16=sb16, bank=bank: e.matmul(ps[:, bank, :], ones_bf, sb16, start=(c == 0),
                                                                          stop=(c == NCH - 1)),
                      reads=s.pages() + CBF, writes=PSB(bank))
            sc.op('act', lambda e, bank=bank: e.activation(out=rstd.ap, in_=ps[:, bank, :], func=AF.Sqrt, bias=1.0,
                                                           scale=1.0 / (D * EPS)), reads=PSB(bank), writes=rstd.pages())
            sc.op('dve', lambda e: e.reciprocal(out=rstd.ap, in_=rstd.ap), reads=rstd.pages(), writes=rstd.pages())
            for c in range(NCH):
                sc.op('dve', lambda e, c=c: e.scalar_tensor_tensor(
                    out=xacc.ap[:, c, :], in0=xacc.ap[:, c, :], scalar=pp[:, fb + c:fb + c + 1], in1=rstd.ap,
                    op0=ALU.mult, op1=ALU.mult), reads=xacc.cp(c) + rstd.pages() + PP, writes=xacc.cp(c))
            for tb in range(4):
                xs = xstage[tb % 2]
                for c4 in range(8):
                    bank = (c4 % 2) * 4 + (c4 // 2) % 4

                    def tr2(e, c4=c4, tb=tb, bank=bank):
                        last = None
                        for q in range(4):
                            last = e.transpose(ps[:, bank, q * 128:(q + 1) * 128],
                                               xacc.ap[:, c4 * 4 + q, tb * 128:(tb + 1) * 128], identf)
                        return last
                    sc.op('pe', tr2, reads=xacc.cp(c4 * 4, 4) + CONST, writes=PSB(bank))
                    copy_op(evac_eng(), xs.ap[:, c4 * 4:(c4 + 1) * 4, :],
                            ps[:, bank, :].rearrange("p (a b) -> p a b", a=4),
                            reads=PSB(bank), writes=xs.pages(c4 * 2048, (c4 + 1) * 2048))
                r0 = t0 + tb * 128
                sc.op('sp', lambda e, xs=xs, r0=r0: e.dma_start(
                    out=out_d[r0:r0 + 128, :], in_=xs.ap.rearrange("p a b -> p (a b)")),
                    reads=xs.pages(), writes=[('out', r0)])

        sc.emit(block)
    return nc


def pack_params(inp, L):
    def pc(v):
        v = np.asarray(v, dtype=np.float32)
        return v.reshape(-1, 128).T
    cols = []
    for l in range(4):
        if l < L:
            cols += [pc(inp["norm1_g"][l]), pc(inp["norm2_g"][l]), pc(inp["gate_b"][l, 0]), pc(inp["gate_b"][l, 1])]
            cols += [pc(inp["conv_w"][l, jj]) for jj in range(4)]
            cols += [pc(inp["conv_b"][l]), pc(inp["b_rgate"][l]), pc(inp["b_igate"][l]), pc(inp["lru_lambda"][l]),
                     pc(inp["subln_g"][l])]
        else:
            cols.append(np.ones((128, PPL), np.float32))
    cols.append(pc(inp["final_g"]))
    return np.ascontiguousarray(np.concatenate(cols, axis=1))


def make_consts():
    c = np.zeros((128, 384), np.float32)
    c[:, 0:128] = np.eye(128, dtype=np.float32)
    c[:, 128:256] = np.triu(np.ones((128, 128), np.float32))
    c[:, 256:384] = 1.0
    return c


def kernel(**inputs):
    inp = {k: np.asarray(v) for k, v in inputs.items()}
    L, NT, NCORES = 4, 4, 8
    nc = build(L, NT)
    pp = pack_params(inp, L)
    consts = make_consts()
    shared = {
        "w_in": inp["w_in"], "w_attn_proj": inp["w_attn_proj"], "w_rec_proj": inp["w_rec_proj"],
        "w_out": inp["w_out"], "w_up": inp["w_up"], "w_down": inp["w_down"],
        "w_rgate": inp["w_rgate"], "w_igate": inp["w_igate"], "pp": pp,
        "lam_qk": np.ascontiguousarray(inp["lam_qk"], dtype=np.float32), "consts": consts,
    }
    in_maps = []
    for c in range(NCORES):
        m = dict(shared)
        m["x"] = np.ascontiguousarray(inp["x"][c], dtype=np.float32)
        in_maps.append(m)
    res = run_bass_kernel_spmd(nc, in_maps, core_ids=list(range(NCORES)))
    return np.stack([np.asarray(r["out"]) for r in res.results], axis=0).astype(np.float32)
```
